# Optimizing a Trainium2 kernel written in Bass

```python
import functools
import jax
import jax.numpy as jnp
from jax import lax
import numpy as np

D_MODEL = 1024
BATCH = 16
SEQ = 2048
DEPTH = 4

CTX_LEN = 256
GRID_W = 64
EPS = 1e-6

RET_HEADS = 4
RET_DK = 64
RET_DV = 128
RET_CHUNK = 64
ROPE_BASE = 10000.0

RWKV_HEADS = 8
RWKV_N = 64
RWKV_DECAY_RANK = 64
RWKV_ICLR_RANK = 64
RWKV_GN_EPS = 64e-5

GLA_HEADS = 4
GLA_DK = 64
GLA_DV = 128
GLA_GATE_RANK = 16
GLA_TAU = 16.0
GLA_CHUNK = 64

RET_W = RET_HEADS * RET_DV
RWKV_W = RWKV_HEADS * RWKV_N
GLA_W = GLA_HEADS * GLA_DV
D_MIX = RET_W + RWKV_W + GLA_W

RET_COLS = 2 * RET_HEADS * RET_DK + RET_W
RWKV_COLS = 3 * RWKV_W + 2 * RWKV_DECAY_RANK + 2 * RWKV_ICLR_RANK
GLA_COLS = 2 * GLA_HEADS * GLA_DK + GLA_W + 2 * GLA_GATE_RANK
GATE_COLS = D_MIX
P_IN = RET_COLS + RWKV_COLS + GLA_COLS + GATE_COLS

kernel_name = "hybrid_retention_rwkv7_gla_prefix_dit"


def rms_norm(x, g):
    xf = x.astype(jnp.float32)
    y = xf * lax.rsqrt(jnp.mean(xf * xf, axis=-1, keepdims=True) + EPS)
    return (y * g.astype(jnp.float32)).astype(x.dtype)


def head_norm(o, gain, bias=None, eps=EPS, center=True):
    of = o.astype(jnp.float32)
    if center:
        of = of - jnp.mean(of, axis=-1, keepdims=True)
    of = of * lax.rsqrt(jnp.mean(of * of, axis=-1, keepdims=True) + eps)
    y = of.reshape(of.shape[:-2] + (-1,)) * gain.astype(jnp.float32)
    if bias is not None:
        y = y + bias.astype(jnp.float32)
    return y.astype(o.dtype)


def split_heads(a, n_heads):
    return a.reshape(a.shape[:-1] + (n_heads, a.shape[-1] // n_heads))


def axial_rope(x, rows, cols):
    half = x.shape[-1] // 2
    nf = half // 2
    inv_freq = 1.0 / (ROPE_BASE ** (jnp.arange(nf, dtype=jnp.float32) / nf))

    def rotate(xp, pos):
        ang = pos.astype(jnp.float32)[:, None] * inv_freq[None, :]
        cos = jnp.cos(ang)[None, :, None, :]
        sin = jnp.sin(ang)[None, :, None, :]
        x1 = xp[..., :nf].astype(jnp.float32)
        x2 = xp[..., nf:].astype(jnp.float32)
        return jnp.concatenate([x1 * cos - x2 * sin, x1 * sin + x2 * cos], axis=-1)

    out = jnp.concatenate([rotate(x[..., :half], rows), rotate(x[..., half:], cols)], axis=-1)
    return out.astype(x.dtype)


def centred_token_shift(p, mu_prev, mu_next):
    zeros = jnp.zeros_like(p[:, :1])
    prev = jnp.concatenate([zeros, p[:, :-1]], axis=1)
    nxt = jnp.concatenate([p[:, 1:], zeros], axis=1)
    return p + mu_prev * (prev - p) + mu_next * (nxt - p)


def retention_chunked(q, k, v, s0, log_gamma):
    dt = v.dtype
    B, T, H, dk = q.shape
    dv = v.shape[-1]
    C = RET_CHUNK
    n = T // C
    qc = q.astype(jnp.float32).reshape(B, n, C, H, dk)
    kc = k.astype(jnp.float32).reshape(B, n, C, H, dk)
    vc = v.astype(jnp.float32).reshape(B, n, C, H, dv)
    pos = jnp.arange(C, dtype=jnp.float32)
    rel = pos[:, None] - pos[None, :]
    decay = jnp.where(rel[None] >= 0, jnp.exp(jnp.maximum(rel, 0.0)[None] * log_gamma[:, None, None]), 0.0)
    scores = jnp.einsum('bnihd,bnjhd->bnhij', qc, kc) * decay
    o_intra = jnp.einsum('bnhij,bnjhe->bnihe', scores, vc)
    zeta = jnp.exp((C - 1.0 - pos)[:, None] * log_gamma[None, :])
    xi = jnp.exp((pos + 1.0)[:, None] * log_gamma[None, :])
    kv = jnp.einsum('bnjhd,bnjhe->nbhde', kc * zeta[:, :, None], vc)
    chunk_decay = jnp.exp(C * log_gamma)[None, :, None, None]

    def step(s, kv_n):
        return chunk_decay * s + kv_n, s

    s_final, s_prev = lax.scan(step, s0, kv)
    o_inter = jnp.einsum('bnihd,nbhde->bnihe', qc * xi[:, :, None], s_prev)
    return (o_intra + o_inter).reshape(B, T, H, dv).astype(dt), s_final


def gla_chunked(q, k, v, log_a, s0):
    dt = v.dtype
    B, T, H, dk = q.shape
    dv = v.shape[-1]
    C = GLA_CHUNK
    n = T // C
    qc = q.astype(jnp.float32).reshape(B, n, C, H, dk)
    kc = k.astype(jnp.float32).reshape(B, n, C, H, dk)
    vc = v.astype(jnp.float32).reshape(B, n, C, H, dv)
    lc = log_a.astype(jnp.float32).reshape(B, n, C, H, dk)
    b = jnp.cumsum(lc, axis=2)
    q_t = qc * jnp.exp(b)
    k_t = kc * jnp.exp(-b)
    mask = jnp.tril(jnp.ones((C, C), dtype=bool))
    scores = jnp.where(mask, jnp.einsum('bnihd,bnjhd->bnhij', q_t, k_t), 0.0)
    o_intra = jnp.einsum('bnhij,bnjhe->bnihe', scores, vc)
    b_last = b[:, :, -1]
    kv = jnp.einsum('bnjhd,bnjhe->nbhde', kc * jnp.exp(b_last[:, :, None] - b), vc)
    a_chunk = jnp.moveaxis(jnp.exp(b_last), 1, 0)[..., None]

    def step(s, inp):
        a_n, kv_n = inp
        return a_n * s + kv_n, s

    s_final, s_prev = lax.scan(step, s0, (a_chunk, kv))
    o_inter = jnp.einsum('bnihd,nbhde->bnihe', q_t, s_prev)
    return (o_intra + o_inter).reshape(B, T, H, dv).astype(dt), s_final


def rwkv7_scan(r, w, k, v, kk, ka, s0):
    dt = r.dtype
    xs = tuple(jnp.moveaxis(a.astype(jnp.float32), 1, 0) for a in (r, w, k, v, kk, ka))

    def step(s, inp):
        r_t, w_t, k_t, v_t, kk_t, ka_t = inp
        sa = jnp.einsum('bhvk,bhk->bhv', s, kk_t)
        s = s * w_t[:, :, None, :] - sa[..., None] * ka_t[:, :, None, :] + v_t[..., None] * k_t[:, :, None, :]
        return s, jnp.einsum('bhvk,bhk->bhv', s, r_t)

    s_final, o = lax.scan(step, s0, xs)
    return jnp.moveaxis(o, 0, 1).astype(dt), s_final


def prefix_bidirectional(fn_f, fn_b, ctx_f, lat_f, ctx_b, lat_b, s0):
    flip = lambda xs: tuple(jnp.flip(a, axis=1) for a in xs)
    oc_f, sc_f = fn_f(*ctx_f, s0)
    ox_f, _ = fn_f(*lat_f, sc_f)
    oc_b, sc_b = fn_b(*flip(ctx_b), s0)
    ox_b, _ = fn_b(*flip(lat_b), sc_b)
    return oc_f + jnp.flip(oc_b, axis=1), ox_f + jnp.flip(ox_b, axis=1)


def split_cols(p):
    offs = [int(o) for o in np.cumsum([RET_COLS, RWKV_COLS, GLA_COLS])]
    return jnp.split(p, offs, axis=-1)


def retention_inputs(p, rows, cols):
    hk = RET_HEADS * RET_DK
    q, k, v = jnp.split(p, [hk, 2 * hk], axis=-1)
    q = split_heads(q, RET_HEADS)
    k = split_heads(k, RET_HEADS) * RET_DK ** -0.5
    v = split_heads(v, RET_HEADS)
    if rows is not None:
        q = axial_rope(q, rows, cols)
        k = axial_rope(k, rows, cols)
    return (q, k, v)


def rwkv_inputs(p, mu, w0, w_up, a0, a_up, k_k, k_a):
    p = centred_token_shift(p, mu[0], mu[1])
    W = RWKV_W
    r, k, v, wd, ad = jnp.split(p, [W, 2 * W, 3 * W, 3 * W + 2 * RWKV_DECAY_RANK], axis=-1)
    wd = wd.reshape(wd.shape[:-1] + (2, RWKV_DECAY_RANK))
    ad = ad.reshape(ad.shape[:-1] + (2, RWKV_ICLR_RANK))
    kk = split_heads((k * k_k).astype(jnp.float32), RWKV_HEADS)
    kk = kk / jnp.maximum(jnp.linalg.norm(kk, axis=-1, keepdims=True), 1e-12)
    r_h = split_heads(r, RWKV_HEADS)
    v_h = split_heads(v, RWKV_HEADS)
    dirs = []
    for d in range(2):
        w_log = -jax.nn.softplus(-(w0[d] + jnp.tanh(wd[..., d, :]) @ w_up[d]).astype(jnp.float32)) - 0.5
        decay = jnp.exp(-jnp.exp(w_log))
        a = jax.nn.sigmoid((a0[d] + ad[..., d, :] @ a_up[d]).astype(jnp.float32))
        k_d = k.astype(jnp.float32) * (1.0 + (a - 1.0) * k_a.astype(jnp.float32))
        a_h = split_heads(a, RWKV_HEADS)
        dirs.append((r_h, split_heads(decay, RWKV_HEADS), split_heads(k_d, RWKV_HEADS), v_h, kk, kk * a_h))
    return dirs


def rwkv_bonus(dirs, r_k):
    terms = [jnp.sum(r.astype(jnp.float32) * k * r_k.astype(jnp.float32), axis=-1, keepdims=True) * v.astype(jnp.float32)
             for (r, _, k, v, _, _) in dirs]
    b = terms[0] + terms[1]
    return b.reshape(b.shape[:-2] + (-1,)).astype(dirs[0][0].dtype)


def gla_inputs(p, a_up, a_b):
    hk = GLA_HEADS * GLA_DK
    q, k, v, ad = jnp.split(p, [hk, 2 * hk, 2 * hk + GLA_W], axis=-1)
    q = split_heads(q, GLA_HEADS) * GLA_DK ** -0.5
    k = split_heads(k, GLA_HEADS)
    v = split_heads(v, GLA_HEADS)
    ad = ad.reshape(ad.shape[:-1] + (2, GLA_GATE_RANK))
    dirs = []
    for d in range(2):
        log_a = jax.nn.log_sigmoid((ad[..., d, :] @ a_up[d] + a_b[d]).astype(jnp.float32)) / GLA_TAU
        dirs.append((q, k, v, split_heads(log_a, GLA_HEADS)))
    return dirs


def token_mixers(pc, px, rows, cols, ret_decay_logit, ret_gn, rwkv_mu, rwkv_w0, rwkv_w_up, rwkv_a0,
                 rwkv_a_up, rwkv_k_k, rwkv_k_a, rwkv_r_k, rwkv_gn_w, rwkv_gn_b, gla_a_up, gla_a_b, gla_gn):
    B = px.shape[0]
    ret_c, rwkv_c, gla_c, gate_c = split_cols(pc)
    ret_x, rwkv_x, gla_x, gate_x = split_cols(px)

    lg = jax.nn.log_sigmoid(ret_decay_logit.astype(jnp.float32))
    qkv_c = retention_inputs(ret_c, None, None)
    qkv_x = retention_inputs(ret_x, rows, cols)
    s0 = jnp.zeros((B, RET_HEADS, RET_DK, RET_DV), jnp.float32)
    a_c, a_x = prefix_bidirectional(functools.partial(retention_chunked, log_gamma=lg[0]),
                                    functools.partial(retention_chunked, log_gamma=lg[1]),
                                    qkv_c, qkv_x, qkv_c, qkv_x, s0)
    a_c = head_norm(a_c, ret_gn)
    a_x = head_norm(a_x, ret_gn)

    in_c = rwkv_inputs(rwkv_c, rwkv_mu, rwkv_w0, rwkv_w_up, rwkv_a0, rwkv_a_up, rwkv_k_k, rwkv_k_a)
    in_x = rwkv_inputs(rwkv_x, rwkv_mu, rwkv_w0, rwkv_w_up, rwkv_a0, rwkv_a_up, rwkv_k_k, rwkv_k_a)
    s0 = jnp.zeros((B, RWKV_HEADS, RWKV_N, RWKV_N), jnp.float32)
    b_c, b_x = prefix_bidirectional(rwkv7_scan, rwkv7_scan, in_c[0], in_x[0], in_c[1], in_x[1], s0)
    b_c = head_norm(b_c, rwkv_gn_w, rwkv_gn_b, eps=RWKV_GN_EPS) + rwkv_bonus(in_c, rwkv_r_k)
    b_x = head_norm(b_x, rwkv_gn_w, rwkv_gn_b, eps=RWKV_GN_EPS) + rwkv_bonus(in_x, rwkv_r_k)

    g_in_c = gla_inputs(gla_c, gla_a_up, gla_a_b)
    g_in_x = gla_inputs(gla_x, gla_a_up, gla_a_b)
    s0 = jnp.zeros((B, GLA_HEADS, GLA_DK, GLA_DV), jnp.float32)
    g_c, g_x = prefix_bidirectional(gla_chunked, gla_chunked, g_in_c[0], g_in_x[0], g_in_c[1], g_in_x[1], s0)
    g_c = head_norm(g_c, gla_gn, center=False)
    g_x = head_norm(g_x, gla_gn, center=False)

    yc = jnp.concatenate([a_c, b_c, g_c], axis=-1) * jax.nn.silu(gate_c)
    yx = jnp.concatenate([a_x, b_x, g_x], axis=-1) * jax.nn.silu(gate_x)
    return yc, yx


def setup_inputs(seed: int = 0) -> dict:
    key = jax.random.key(seed)
    ks = iter(jax.random.split(key, 32))
    nrm = lambda shape, scale: jax.random.normal(next(ks), shape, jnp.float32) * scale
    L, D = DEPTH, D_MODEL
    gamma = 1.0 - 2.0 ** (-5.0 - jnp.arange(RET_HEADS, dtype=jnp.float32))
    base_logit = jnp.log(gamma) - jnp.log1p(-gamma)
    return {
        "x": nrm((BATCH, SEQ, D), 1.0),
        "c": nrm((BATCH, D), 1.0),
        "ctx": nrm((BATCH, CTX_LEN, D), 1.0),
        "c_ctx": nrm((D,), 1.0),
        "w_mod": nrm((L, D, 3 * D), 0.5 * D ** -0.5),
        "b_mod": nrm((L, 3 * D), 0.02),
        "g_pre": 1.0 + nrm((L, D), 0.05),
        "g_post": 1.0 + nrm((L, D), 0.05),
        "w_in": nrm((L, D, P_IN), D ** -0.5),
        "w_out": nrm((L, D_MIX, D), D_MIX ** -0.5),
        "ret_decay_logit": base_logit[None, None, :] + nrm((L, 2, RET_HEADS), 0.1),
        "ret_gn": 1.0 + nrm((L, RET_W), 0.05),
        "rwkv_mu": 0.3 + nrm((L, 2, RWKV_COLS), 0.1),
        "rwkv_w0": -0.5 + nrm((L, 2, RWKV_W), 0.5),
        "rwkv_w_up": nrm((L, 2, RWKV_DECAY_RANK, RWKV_W), 0.5 * RWKV_DECAY_RANK ** -0.5),
        "rwkv_a0": nrm((L, 2, RWKV_W), 0.5),
        "rwkv_a_up": nrm((L, 2, RWKV_ICLR_RANK, RWKV_W), 0.5 * RWKV_ICLR_RANK ** -0.5),
        "rwkv_k_k": 0.85 + nrm((L, RWKV_W), 0.05),
        "rwkv_k_a": 1.0 + nrm((L, RWKV_W), 0.05),
        "rwkv_r_k": nrm((L, RWKV_HEADS, RWKV_N), 0.1),
        "rwkv_gn_w": 1.0 + nrm((L, RWKV_W), 0.05),
        "rwkv_gn_b": nrm((L, RWKV_W), 0.02),
        "gla_a_up": nrm((L, 2, GLA_GATE_RANK, GLA_HEADS * GLA_DK), GLA_GATE_RANK ** -0.5),
        "gla_a_b": 1.0 + nrm((L, 2, GLA_HEADS * GLA_DK), 0.5),
        "gla_gn": 1.0 + nrm((L, GLA_W), 0.05),
    }


def reference(x, c, ctx, c_ctx, w_mod, b_mod, g_pre, g_post, w_in, w_out, ret_decay_logit, ret_gn,
              rwkv_mu, rwkv_w0, rwkv_w_up, rwkv_a0, rwkv_a_up, rwkv_k_k, rwkv_k_a, rwkv_r_k,
              rwkv_gn_w, rwkv_gn_b, gla_a_up, gla_a_b, gla_gn):
    n_tok = x.shape[1]
    ROWS = n_tok // GRID_W
    rows = jnp.repeat(jnp.arange(ROWS, dtype=jnp.int32), GRID_W)
    cols = jnp.tile(jnp.arange(GRID_W, dtype=jnp.int32), ROWS)
    s_ctx = ctx
    silu_c = jax.nn.silu(c)
    silu_cc = jax.nn.silu(c_ctx)
    for l in range(DEPTH):
        shift_x, scale_x, gate_x = jnp.split(silu_c @ w_mod[l] + b_mod[l], 3, axis=-1)
        shift_c, scale_c, gate_c = jnp.split(silu_cc @ w_mod[l] + b_mod[l], 3, axis=-1)
        hx = rms_norm(x, g_pre[l]) * (1.0 + scale_x[:, None]) + shift_x[:, None]
        hc = rms_norm(s_ctx, g_pre[l]) * (1.0 + scale_c) + shift_c
        px = hx @ w_in[l]
        pc = hc @ w_in[l]
        yc, yx = token_mixers(pc, px, rows, cols, ret_decay_logit[l], ret_gn[l], rwkv_mu[l], rwkv_w0[l],
                              rwkv_w_up[l], rwkv_a0[l], rwkv_a_up[l], rwkv_k_k[l], rwkv_k_a[l], rwkv_r_k[l],
                              rwkv_gn_w[l], rwkv_gn_b[l], gla_a_up[l], gla_a_b[l], gla_gn[l])
        x = x + gate_x[:, None] * rms_norm(yx @ w_out[l], g_post[l])
        if l < DEPTH - 1:
            s_ctx = s_ctx + gate_c * rms_norm(yc @ w_out[l], g_post[l])
    return x
```

```python
import numpy as np
import concourse.bass as bass
import concourse.mybir as mybir
from concourse.bass_utils import run_bass_kernel_spmd

F32 = mybir.dt.float32
BF16 = mybir.dt.bfloat16
AF = mybir.ActivationFunctionType
ALU = mybir.AluOpType

ENGS = ("sync", "act", "dve", "pool", "pe")


class _Rec:
    __slots__ = ("w", "r")

    def __init__(self):
        self.w = None
        self.r = {}


class Tl:
    def __init__(self, h, name, excl=False):
        self.h = h
        self.name = name
        self.excl = excl
        self.whole = _Rec()
        self.sub = {}

    def __getitem__(self, idx):
        return self.h[idx]


class Op:
    __slots__ = ("eng", "idx", "fn", "deps", "is_dma", "signal", "cnt", "slot", "target")

    def __init__(self, eng, idx, fn, is_dma):
        self.eng = eng
        self.idx = idx
        self.fn = fn
        self.deps = []
        self.is_dma = is_dma
        self.signal = False
        self.cnt = 0
        self.slot = None
        self.target = None


class Prog:
    NSLOT = 12

    def __init__(self, nc):
        self.nc = nc
        self.ops = {e: [] for e in ENGS}
        self.ndma = {e: 0 for e in ENGS}
        self.out_dmas = []

    def sb(self, name, shape, dt):
        return Tl(self.nc.alloc_sbuf_tensor(name, list(shape), dt), name)

    def ps(self, name, shape, dt=F32):
        return Tl(self.nc.alloc_psum_tensor(name, list(shape), dt), name, excl=True)

    def dram(self, name, shape, dt, kind="Internal"):
        return Tl(self.nc.dram_tensor(name, list(shape), dt, kind=kind), name)

    @staticmethod
    def _norm(a):
        if isinstance(a, Tl):
            return a, None
        return a

    def op(self, eng, fn, reads=(), writes=(), dma=False):
        lst = self.ops[eng]
        nr, nw = [], []
        for a in reads:
            t, k = self._norm(a)
            (nw if t.excl else nr).append((t, None) if t.excl else (t, k))
        for a in writes:
            t, k = self._norm(a)
            nw.append((t, None) if t.excl else (t, k))
        reads, writes = nr, nw
        o = Op(eng, len(lst), fn, dma)
        deps = set()
        for a in reads:
            t, k = self._norm(a)
            recs = [t.whole] if k is None else [t.whole, t.sub.get(k)]
            if k is None:
                recs += list(t.sub.values())
            for rc in recs:
                if rc is not None and rc.w is not None:
                    deps.add(rc.w)
        for a in writes:
            t, k = self._norm(a)
            recs = [t.whole] if k is None else [t.whole, t.sub.get(k)]
            if k is None:
                recs += list(t.sub.values())
            for rc in recs:
                if rc is None:
                    continue
                if rc.w is not None:
                    deps.add(rc.w)
                for x in rc.r.values():
                    deps.add(x)
        for d in deps:
            if d is o:
                continue
            if d.eng == eng and not d.is_dma and not dma and eng == "pe":
                continue
            o.deps.append(d)
            d.signal = True
        for a in reads:
            t, k = self._norm(a)
            rc = t.whole if k is None else t.sub.setdefault(k, _Rec())
            rc.r[eng if not dma else (eng, "dma", o.idx)] = o
            if dma:
                pass
        for a in writes:
            t, k = self._norm(a)
            if k is None:
                t.sub = {}
                t.whole = _Rec()
                t.whole.w = o
            else:
                rc = t.sub.setdefault(k, _Rec())
                rc.w = o
                rc.r = {}
        lst.append(o)
        return o

    def dma(self, eng, out_ap, in_ap, reads, writes, is_out=False, **kw):
        o = self.op(eng, lambda e: e.dma_start(out=out_ap, in_=in_ap, **kw), reads, writes, dma=True)
        if is_out:
            self.out_dmas.append(o)
        return o

    def emit(self):
        nc = self.nc
        sems = {e: nc.alloc_semaphore("sem_" + e) for e in ENGS}
        dsems = {}
        for e in ENGS:
            if any(o.is_dma for o in self.ops[e]):
                dsems[e] = [nc.alloc_semaphore("dsem_%s_%d" % (e, i)) for i in range(self.NSLOT)]
        for e in ENGS:
            c = 0
            nd = 0
            for o in self.ops[e]:
                if o.is_dma:
                    o.slot = nd % self.NSLOT
                    o.target = 16 * (nd // self.NSLOT + 1)
                    nd += 1
                else:
                    if o.signal:
                        c += 1
                    o.cnt = c
        engobj = {"sync": "sync", "act": "scalar", "dve": "vector", "pool": "gpsimd", "pe": "tensor"}
        prog = self

        def run_engine(ename, e):
            waited = {}

            def wait(sem, key, val):
                if waited.get(key, 0) >= val:
                    return
                waited[key] = val
                e.wait_ge(sem, val)

            for o in prog.ops[ename]:
                for d in o.deps:
                    if d.is_dma:
                        wait(dsems[d.eng][d.slot], (d.eng, d.slot), d.target)
                    else:
                        wait(sems[d.eng], d.eng, d.cnt)
                if o.is_dma:
                    if o.target > 16:
                        wait(dsems[ename][o.slot], (ename, o.slot), o.target - 16)
                    ins = o.fn(e)
                    ins.then_inc(dsems[ename][o.slot], 16)
                else:
                    ins = o.fn(e)
                    if o.signal:
                        ins.then_inc(sems[ename], 1)
            if ename == "sync":
                for o in prog.out_dmas:
                    wait(dsems[o.eng][o.slot], (o.eng, o.slot), o.target)

        with nc.Block() as block:
            @block.sync
            def _(e):
                run_engine("sync", e)

            @block.scalar
            def _(e):
                run_engine("act", e)

            @block.vector
            def _(e):
                run_engine("dve", e)

            @block.gpsimd
            def _(e):
                run_engine("pool", e)

            @block.tensor
            def _(e):
                run_engine("pe", e)


import math

AX = mybir.AxisListType

D = 1024
KD = 8
PIN = 5408
C_RV, C_RW, C_GQ, C_GV, C_GAD, C_GATE = 512, 1024, 2816, 3328, 3840, 3872
PF_RQ, PF_RK, PF_RW, PF_GQ, PF_GK, PF_GATE, PF_ROWS = 0, 256, 512, 2304, 2560, 2816, 4352
NTAB = 80


class Ring:
    def __init__(self, tiles):
        self.t = tiles
        self.i = 0

    def next(self):
        x = self.t[self.i % len(self.t)]
        self.i += 1
        return x


class Arena:
    def __init__(self, P, nbytes, name, junk):
        self.P = P
        self.h = P.nc.alloc_sbuf_tensor(name, [128, nbytes // 4], F32)
        self.n = nbytes
        self.off = 0
        self.live = []
        self.bar = None
        self.junk = junk

    def reset(self, extra=()):
        tl = self.live + list(extra)
        junk = self.junk
        self.bar = self.P.op("pool", lambda e: e.memset(junk[0:1, 0:4], 0.0), [], tl + [junk])
        self.live = []
        self.off = 0
        for t in extra:
            t.whole.w = self.bar

    def alloc(self, name, shape, dt):
        esz = 2 if dt == BF16 else 4
        n = 1
        for s in shape[1:]:
            n *= s
        nb = (n * esz + 31) // 32 * 32
        ap = self.h[:, self.off // 4:(self.off + nb) // 4]
        if dt == BF16:
            ap = ap.bitcast(BF16)
        ap = ap[:, 0:n]
        if len(shape) > 2:
            names = ["a%d" % i for i in range(len(shape) - 1)]
            kw = {names[i]: shape[i + 1] for i in range(len(names) - 1)}
            ap = ap.rearrange("p (%s) -> p %s" % (" ".join(names), " ".join(names)), **kw)
        self.off += nb
        assert self.off <= self.n, (name, self.off, self.n)
        t = Tl(ap, name)
        t.whole.w = self.bar
        self.live.append(t)
        return t


def host_consts(SEQ):
    c = {}
    idx = np.arange(128)
    c["ident"] = np.eye(128, dtype=np.float32)
    j = idx[:, None]
    i = idx[None, :]
    mg = np.zeros((128, 2, 256), np.float32)
    mg[:, 0, :] = np.tile((j <= i).astype(np.float32), (1, 2))
    mg[:, 1, :] = np.tile((j >= i).astype(np.float32), (1, 2))
    c["maskg"] = mg
    tg = np.zeros((128, 2, 128), np.float32)
    tg[:, 0] = (j <= i) * (-1.0 / 16)
    tg[:, 1] = (j >= i) * (-1.0 / 16)
    c["trig"] = tg
    cw = -np.exp(-0.5)
    same = (j // 64) == (i // 64)
    tr = np.zeros((128, 2, 384), np.float32)
    tr[:, 0, 0:128] = same & (j <= i)
    tr[:, 0, 128:256] = same & (j < i)
    tr[:, 0, 256:384] = same & (j > i)
    tr[:, 1, 0:128] = same & (j >= i)
    tr[:, 1, 128:256] = same & (j > i)
    tr[:, 1, 256:384] = same & (j < i)
    c["trir"] = (tr * cw).astype(np.float32)
    a = (idx % 64)[:, None]
    b = (idx % 64)[None, :]
    mk = np.zeros((128, 2, 5, 128), np.float32)
    mk[:, 0, 0] = -1.0 * (same & (a < b))
    mk[:, 0, 1] = -1.0 * (same & (a <= b))
    mk[:, 0, 2] = same & (a < b)
    mk[:, 0, 3] = same & (a <= b)
    mk[:, 0, 4] = -1.0 * (same & (b < a))
    mk[:, 1, 0] = -1.0 * (same & (a > b))
    mk[:, 1, 1] = -1.0 * (same & (a >= b))
    mk[:, 1, 2] = same & (a > b)
    mk[:, 1, 3] = same & (a >= b)
    mk[:, 1, 4] = -1.0 * (same & (b > a))
    c["mk"] = mk
    hs = np.zeros((128, 2), np.float32)
    hs[:64, 0] = 1
    hs[64:, 1] = 1
    c["hsel"] = hs
    c["blk1"] = same.astype(np.float32)
    rm = np.zeros((128, 128), np.float32)
    for base in range(0, 128, 32):
        for q in range(16):
            rm[base + 16 + q, base + q] = -1.0
            rm[base + q, base + 16 + q] = 1.0
    c["rm"] = rm
    t = np.arange(SEQ)
    rows = (t // 64).astype(np.float32)
    cols = (t % 64).astype(np.float32)
    nf = 16
    inv = (np.float32(1.0) / (np.float32(10000.0) ** (np.arange(nf, dtype=np.float32) / np.float32(nf)))).astype(np.float32)
    cos = np.zeros((128, SEQ), np.float32)
    sin = np.zeros((128, SEQ), np.float32)
    for p in range(128):
        dd = p % 64
        pos = rows if dd < 32 else cols
        ang = (pos * inv[dd % 16]).astype(np.float32)
        cos[p] = np.cos(ang)
        sin[p] = np.sin(ang)
    c["ropec"] = cos
    c["ropes"] = sin
    pos = np.zeros((128, 2, 128), np.float32)
    pos[:, 0, :] = np.arange(128) + 1
    pos[:, 1, :] = 128 - np.arange(128)
    c["pos"] = pos
    return {k: np.ascontiguousarray(v, dtype=np.float32) for k, v in c.items()}


def host_params(inp, L):
    def fm(v, n):
        return np.asarray(v, np.float32).reshape(n, 128).T
    tab = np.zeros((L, 128, NTAB), np.float32)
    rows = np.zeros((L, 4096), np.float32)
    for l in range(L):
        tab[l, :, 0:8] = fm(inp["g_pre"][l], 8)
        tab[l, :, 8:16] = fm(inp["b_mod"][l, 0:1024], 8)
        tab[l, :, 16:24] = fm(inp["b_mod"][l, 1024:2048], 8)
        tab[l, :, 24:38] = fm(inp["rwkv_mu"][l, 0], 14)
        tab[l, :, 38:52] = fm(inp["rwkv_mu"][l, 1], 14)
        tab[l, :, 52:56] = fm(inp["rwkv_k_k"][l], 4)
        tab[l, :, 56:60] = fm(inp["rwkv_k_a"][l], 4)
        tab[l, :, 60:64] = fm(inp["rwkv_r_k"][l].reshape(512), 4)
        tab[l, :, 64:68] = fm(inp["rwkv_a0"][l, 0], 4)
        tab[l, :, 68:72] = fm(inp["rwkv_a0"][l, 1], 4)
        for d in range(2):
            for hp in range(2):
                tab[l, 0:64, 72 + d * 2 + hp] = inp["ret_decay_logit"][l, d, hp * 2]
                tab[l, 64:128, 72 + d * 2 + hp] = inp["ret_decay_logit"][l, d, hp * 2 + 1]
        rows[l, 0:512] = inp["ret_gn"][l]
        rows[l, 512:1024] = inp["rwkv_gn_w"][l]
        rows[l, 1024:1536] = inp["gla_gn"][l]
        rows[l, 1536:2048] = inp["rwkv_gn_b"][l]
        rows[l, 2048:3072] = inp["b_mod"][l, 2048:3072]
        rows[l, 3072:4096] = inp["g_post"][l]
    wup = np.concatenate([inp["rwkv_w_up"][:L], inp["rwkv_w0"][:L, :, None, :]], axis=2).astype(np.float32)
    aup = np.ascontiguousarray(inp["rwkv_a_up"][:L], np.float32)
    aupg = np.concatenate([inp["gla_a_up"][:L], inp["gla_a_b"][:L, :, None, :]], axis=2).astype(np.float32)
    return dict(tab=tab, rows=rows, wup=np.ascontiguousarray(wup), aup=aup, aupg=np.ascontiguousarray(aupg))


def build(cfg):
    SEQ, CTX, L, NB = cfg["SEQ"], cfg["CTX"], cfg["L"], cfg["NB"]
    dbg = cfg.get("dbg", False)
    stop_after = cfg.get("stop_after", None)
    T = SEQ + CTX
    NT = T // 128
    CT = CTX // 128
    NV = NB + 1
    nc = bass.Bass("TRN2", target_bir_lowering=False)
    P = Prog(nc)
    EI = "ExternalInput"
    x_in = P.dram("x", [NB, SEQ, D], F32, EI)
    ctx_in = P.dram("ctx", [NB, CTX, D], F32, EI)
    cT_in = P.dram("cT", [128, KD, NV], F32, EI)
    wmod = P.dram("w_mod", [L, D, 3 * D], F32, EI)
    win = P.dram("w_in", [L, D, PIN], F32, EI)
    wout = P.dram("w_out", [L, 1536, D], F32, EI)
    tab_in = P.dram("tab", [L, 128, NTAB], F32, EI)
    rows_in = P.dram("rows", [L, 4096], F32, EI)
    wup_in = P.dram("wup", [L, 2, 65, 512], F32, EI)
    aup_in = P.dram("aup", [L, 2, 64, 512], F32, EI)
    aupg_in = P.dram("aupg", [L, 2, 17, 256], F32, EI)
    cst = {}
    for nm, shp in (("ident", [128, 128]), ("maskg", [128, 2, 256]), ("trig", [128, 2, 128]), ("trir", [128, 2, 384]),
                    ("mk", [128, 2, 5, 128]), ("hsel", [128, 2]), ("blk1", [128, 128]), ("rm", [128, 128]),
                    ("ropec", [128, SEQ]), ("ropes", [128, SEQ]), ("pos", [128, 2, 128])):
        cst[nm] = P.dram("c_" + nm, shp, F32, EI)
    okind = "ExternalOutput"
    y_out = P.dram("y", [NB, SEQ, D], F32, okind)
    skind = okind if dbg else "Internal"
    xs = P.dram("xs", [NB, T, D], F32, skind)
    PF = P.dram("PF", [NB, PF_ROWS, T], BF16, skind)
    ADG = P.dram("ADG", [NB, 32, T], F32, skind)
    VT = P.dram("VT", [NB, T, 1536], BF16, skind)
    RW = P.dram("RW", [NB, 1664, T], BF16, skind)
    RWF = P.dram("RWF", [NB, 128, T], F32, skind)
    OD = [P.dram("OD%d" % d, [NB, T, 1536], BF16, skind) for d in range(2)]
    CF = [P.dram("CF%d" % d, [NB, T, 8], F32, skind) for d in range(2)]
    GD = P.dram("GD", [L, NV, D], F32, skind)

    WIN = P.sb("WIN", [128, KD, PIN], BF16)
    WOUT = P.sb("WOUT", [128, 12, D], BF16)
    IDB = P.sb("IDB", [128, 128], BF16)
    MASKG = P.sb("MASKG", [128, 2, 256], BF16)
    TRIG = P.sb("TRIG", [128, 2, 128], F32)
    TRIR = P.sb("TRIR", [128, 2, 384], F32)
    MK = P.sb("MK", [128, 2, 5, 128], BF16)
    HSEL = P.sb("HSEL", [128, 2], BF16)
    BLK1 = P.sb("BLK1", [128, 128], BF16)
    RMB = P.sb("RMB", [128, 128], BF16)
    POS = P.sb("POS", [128, 2, 128], F32)
    TAB = P.sb("TAB", [128, NTAB], F32)
    TAB2 = P.sb("TAB2", [128, 32], F32)
    ROWS = P.sb("ROWS", [128, 2048], F32)
    GROW = P.sb("GROW", [128, D], F32)
    WUP = P.sb("WUP", [128, 2, 512], F32)
    AUP = P.sb("AUP", [128, 2, 512], BF16)
    AUPG = P.sb("AUPG", [128, 2, 256], F32)
    REB = P.sb("REB", [128, 2, 2, 128], F32)
    RENB = P.sb("RENB", [128, 2, 2, 128], F32)
    ABF = P.sb("ABF", [128, 2, KD, NV], F32)
    SC = P.sb("SC", [128, KD, NV], F32)
    ONES = P.sb("ONES", [128, 1], F32)
    EPS = P.sb("EPS", [128, 4], F32)
    JUNK = P.sb("JUNK", [128, 8], F32)
    BK = [P.ps("BK%d" % i, [128, 512], F32) for i in range(8)]

    def bkf(i):
        return BK[i][:, :]

    def bkb(i):
        return BK[i][:, :].bitcast(BF16)

    _rem = int(nc.sbuf_bytes_remaining)
    _asz = ((_rem - 2048) // 1024) * 1024
    if cfg.get("dbg"):
        print("sbuf remaining", _rem, "arena", _asz)
    AR = Arena(P, _asz, "ARENA", JUNK)
    AR2 = Arena(P, 4, "ARENA2dummy", JUNK)
    AR2.h = WIN.h[:, :, :].rearrange("p k c -> p (k c)").bitcast(F32)
    AR2.n = KD * PIN * 2

    def TT(eng, out, in0, in1, op, R, W):
        P.op(eng, lambda e: e.tensor_tensor(out=out, in0=in0, in1=in1, op=op), R, W)

    def TS(eng, out, in0, s1, s2, op0, op1, R, W):
        if s2 is None:
            P.op(eng, lambda e: e.tensor_scalar(out=out, in0=in0, scalar1=s1, scalar2=None, op0=op0), R, W)
        else:
            P.op(eng, lambda e: e.tensor_scalar(out=out, in0=in0, scalar1=s1, scalar2=s2, op0=op0, op1=op1), R, W)

    def STT(out, in0, sc, in1, op0, op1, R, W):
        P.op("dve", lambda e: e.scalar_tensor_tensor(out=out, in0=in0, scalar=sc, in1=in1, op0=op0, op1=op1), R, W)

    def ACT(out, in_, func, R, W, bias=None, scale=1.0, accum=None):
        kw = {}
        if bias is not None:
            kw["bias"] = bias
        if accum is not None:
            kw["accum_out"] = accum
        P.op("act", lambda e: e.activation(out=out, in_=in_, func=func, scale=scale, **kw), R, W)

    def MM(out, lhsT, rhs, R, W, start=True, stop=True):
        P.op("pe", lambda e: e.matmul(out, lhsT=lhsT, rhs=rhs, start=start, stop=stop), R, W)

    def TRP(out, in_, R, W):
        idb = IDB[:, :]
        P.op("pe", lambda e: e.transpose(out, in_, idb), list(R) + [IDB], W)

    def CP(eng, out, in_, R, W):
        if eng == "act":
            P.op("act", lambda e: e.copy(out=out, in_=in_), R, W)
        else:
            P.op(eng, lambda e: e.tensor_copy(out=out, in_=in_), R, W)

    def MS(eng, ap, val, W):
        P.op(eng, lambda e: e.memset(ap, val), [], W)

    def LD(out, in_, R, W, eng="sync", **kw):
        P.dma(eng, out, in_, R, W, **kw)

    def STO(out, in_, R, W, is_out=False, eng="pool"):
        P.dma(eng, out, in_, R, W, is_out=is_out)

    for (tl, nm) in ((IDB, "ident"), (MASKG, "maskg"), (MK, "mk"), (HSEL, "hsel"), (BLK1, "blk1"), (RMB, "rm")):
        full = tuple(slice(None) for _ in tl.h.shape)
        P.dma("pool", tl[full], cst[nm][full], [cst[nm]], [tl])
    for (tl, nm) in ((TRIG, "trig"), (TRIR, "trir"), (POS, "pos")):
        full = tuple(slice(None) for _ in tl.h.shape)
        LD(tl[full], cst[nm][full], [cst[nm]], [tl])
    MS("dve", ONES[:, :], 1.0, [ONES])
    MS("dve", EPS[:, 0:1], 1e-6, [(EPS, 0)])
    MS("dve", EPS[:, 1:2], 64e-5, [(EPS, 1)])
    MS("dve", EPS[:, 2:3], 0.0, [(EPS, 2)])
    MS("dve", JUNK[:, :], 0.0, [JUNK])
    MS("pool", AUP[:, :, :], 0.0, [AUP])
    MS("pool", AUPG[:, :, :], 0.0, [AUPG])
    LD(SC[:, :, :], cT_in[:, :, :], [cT_in], [SC])
    ACT(SC[:, :, :], SC[:, :, :], AF.Silu, [SC], [SC])

    groups = []
    for a in range(0, CT, 4):
        groups.append((list(range(a, min(a + 4, CT))), True))
    for a in range(CT, NT, 4):
        groups.append((list(range(a, min(a + 4, NT))), False))

    def order(d):
        if d == 0:
            return list(range(NT))
        return list(range(CT - 1, -1, -1)) + list(range(NT - 1, CT - 1, -1))

    def kPF(s, tt, ch):
        return (PF, (s, tt, ch))

    skip = cfg.get("skip", ())

    def layer_setup(l):
        AR.reset(BK)
        AR2.reset([WIN])
        LD(TAB[:, :], tab_in[l, :, :], [tab_in], [TAB])
        LD(ROWS[:, :].rearrange("p (o n) -> p o n", o=1), rows_in[l:l + 1, 0:2048].partition_broadcast(128), [rows_in], [ROWS])
        LD(WUP[0:65, :, :], wup_in[l].rearrange("d r c -> r d c"), [wup_in], [WUP])
        P.dma("pool", AUP[0:64, 0, :], aup_in[l, 0], [aup_in], [AUP])
        P.dma("pool", AUP[64:128, 1, :], aup_in[l, 1], [aup_in], [AUP])
        LD(AUPG[0:16, :, :], aupg_in[l, :, 0:16, :].rearrange("d r c -> r d c"), [aupg_in], [AUPG])
        LD(AUPG[32:33, :, :], aupg_in[l, :, 16:17, :].rearrange("d r c -> r d c"), [aupg_in], [AUPG])
        if "setup" in skip:
            return
        wv = win[l].rearrange("(k p) c -> p k c", p=128)
        for c0 in range(0, PIN, 1024):
            c1 = min(PIN, c0 + 1024)
            P.dma("pool", WIN[:, :, c0:c1], wv[:, :, c0:c1], [win], [WIN])
        wo = wout[l].rearrange("(k p) c -> p k c", p=128)
        P.dma("pool", WOUT[:, :, :], wo[:, :, :], [wout], [WOUT])
        TS("dve", TAB2[:, 0:14], TAB[:, 24:38], -1.0, 1.0, ALU.mult, ALU.add, [TAB], [(TAB2, "c0")])
        TT("dve", TAB2[:, 0:14], TAB2[:, 0:14], TAB[:, 38:52], ALU.subtract, [TAB, (TAB2, "c0")], [(TAB2, "c0")])
        TS("dve", TAB2[:, 14:18], TAB[:, 56:60], -1.0, 1.0, ALU.mult, ALU.add, [TAB], [(TAB2, "omka")])
        ACT(TAB2[:, 26:30], TAB[:, 72:76], AF.Exp, [TAB], [(TAB2, "t")], scale=-1.0)
        ACT(TAB2[:, 22:26], TAB2[:, 26:30], AF.Ln, [(TAB2, "t"), ONES], [(TAB2, "nlg")], bias=ONES[:, 0:1])
        TS("dve", TAB2[:, 18:22], TAB2[:, 22:26], -1.0, None, ALU.mult, None, [(TAB2, "nlg")], [(TAB2, "lg")])
        for d in range(2):
            for hp in range(2):
                ci = d * 2 + hp
                ACT(REB[:, d, hp, :], POS[:, d, :], AF.Exp, [POS, (TAB2, "lg")], [(REB, ci)], scale=TAB2[:, 18 + ci:19 + ci])
                ACT(RENB[:, d, hp, :], POS[:, d, :], AF.Exp, [POS, (TAB2, "nlg")], [(RENB, ci)], scale=TAB2[:, 22 + ci:23 + ci])
        if "mod" in skip:
            return
        WM = Ring([AR.alloc("WM%d" % i, [128, KD, 256], F32) for i in range(2)])
        R1 = AR.alloc("R1", [128, 2048], F32)
        GR = AR.alloc("GR", [128, NV, 1024], F32)
        SCB = AR.alloc("SCB", [128, KD, NV, 128], F32)
        CP("dve", SCB[:, :, :, :], SC[:, :, :].rearrange("p k (v o) -> p k v o", o=1).broadcast_to([128, KD, NV, 128]), [SC], [SCB])
        LD(R1[0:1, :], rows_in[l:l + 1, 2048:4096], [rows_in], [R1])
        wmv = wmod[l].rearrange("(k p) c -> p k c", p=128)
        psm = BK[0]
        psmv = bkf(0)[:, 0:16 * NV].rearrange("p (c v) -> p c v", v=NV)
        for pc in range(12):
            wm = WM.next()
            LD(wm[:, :, :], wmv[:, :, pc * 256:(pc + 1) * 256], [wmod], [wm])
            if pc < 8:
                for cc in range(2):
                    ch = pc * 2 + cc
                    for k in range(KD):
                        MM(psmv[:, ch, :], wm[:, k, cc * 128:(cc + 1) * 128], SC[:, k, :], [wm, SC], [(psm, "m")], start=(k == 0), stop=(k == KD - 1))
            else:
                cg = pc - 8
                for v in range(NV if "gate" not in skip else 0):
                    pg = bkf(1 + (v % 2))
                    for k in range(KD):
                        MM(pg[:, 0:256], SCB[:, k, v, :], wm[:, k, :], [wm, SCB], [BK[1 + (v % 2)]], start=(k == 0), stop=(k == KD - 1))
                    cs_ = slice(cg * 256, (cg + 1) * 256)
                    cs2 = slice(1024 + cg * 256, 1024 + (cg + 1) * 256)
                    TT("dve", GR[0:1, v, cs_], pg[0:1, 0:256], R1[0:1, cs_], ALU.add, [BK[1 + (v % 2)], R1], [(GR, (v, cg))])
                    TT("dve", GR[0:1, v, cs_], GR[0:1, v, cs_], R1[0:1, cs2], ALU.mult, [(GR, (v, cg)), R1], [(GR, (v, cg))])
        TT("dve", ABF[:, 1, :, :], psmv[:, 0:8, :], TAB[:, 8:16].rearrange("p (k o) -> p k o", o=1).broadcast_to([128, KD, NV]), ALU.add, [(psm, "m"), TAB], [(ABF, "B")])
        TT("dve", ABF[:, 0, :, :], psmv[:, 8:16, :], TAB[:, 16:24].rearrange("p (k o) -> p k o", o=1).broadcast_to([128, KD, NV]), ALU.add, [(psm, "m"), TAB], [(ABF, "A")])
        STT(ABF[:, 0, :, :], ABF[:, 0, :, :], 1.0, TAB[:, 0:8].rearrange("p (k o) -> p k o", o=1).broadcast_to([128, KD, NV]), ALU.add, ALU.mult, [(ABF, "A"), TAB], [(ABF, "A")])
        for v in range(NV):
            STO(GD[l, v:v + 1, :], GR[0:1, v, :], [GR], [(GD, (l, v))])

    def stageA(l, s):
        AR.reset(BK)
        xin = Ring([AR.alloc("xin%d" % i, [128, D], F32) for i in range(2)])
        xn = Ring([AR.alloc("xn%d" % i, [128, D], BF16) for i in range(2)])
        hT = Ring([AR.alloc("hT%d" % i, [128, KD, 512], BF16) for i in range(2)])
        stg = Ring([AR.alloc("stg%d" % i, [128, 512], BF16) for i in range(4)])
        stf = AR.alloc("stf", [128, 512], F32)
        cosT = AR.alloc("cosT", [128, 512], F32)
        sinT = AR.alloc("sinT", [128, 512], F32)
        rt1 = AR.alloc("rt1", [128, 512], F32)
        rt2 = AR.alloc("rt2", [128, 512], F32)
        sst = Ring([AR.alloc("sst%d" % i, [128, 4], F32) for i in range(2)])
        junkA = AR.alloc("junkA", [128, D], BF16)
        psA = Ring([2, 3, 4, 5])
        evi = [0]
        hbuf = {}

        def prepH(gi):
            tiles, isctx = groups[gi]
            N = 128 * len(tiles)
            t0 = tiles[0] * 128
            h = hT.next()
            hbuf[gi] = h
            vec = NB if isctx else s
            for j, tt in enumerate(tiles):
                xi = xin.next()
                xb = xn.next()
                st = sst.next()
                if l == 0:
                    if isctx:
                        src, srct = ctx_in[s, tt * 128:(tt + 1) * 128, :], ctx_in
                    else:
                        src, srct = x_in[s, (tt - CT) * 128:(tt - CT + 1) * 128, :], x_in
                    LD(xi[:, :], src, [srct], [xi])
                else:
                    LD(xi[:, :], xs[s, tt * 128:(tt + 1) * 128, :], [(xs, (s, tt))], [xi])
                if "norm1" not in skip:
                    ACT(junkA[:, :], xi[:, :], AF.Square, [xi], [junkA, (st, 0)], accum=st[:, 0:1])
                else:
                    MS("dve", st[:, 0:1], 1024.0, [(st, 0)])
                if "norm2" not in skip:
                    ACT(st[:, 1:2], st[:, 0:1], AF.Sqrt, [(st, 0), (EPS, 0)], [(st, 1)], bias=EPS[:, 0:1], scale=1.0 / D)
                else:
                    MS("dve", st[:, 1:2], 1.0, [(st, 1)])
                P.op("dve", lambda e, o=st[:, 2:3], i_=st[:, 1:2]: e.reciprocal(out=o, in_=i_), [(st, 1)], [(st, 2)])
                TS("dve", xb[:, :], xi[:, :], st[:, 2:3], None, ALU.mult, None, [xi, (st, 2)], [xb])
                pb_ = (tt % 2)
                ptv = bkb(pb_).rearrange("p (k t) -> p k t", k=KD)
                for k in range(KD if "tr" not in skip else 0):
                    TRP(ptv[:, k, :], xb[:, k * 128:(k + 1) * 128], [xb], [(BK[pb_], k)])
                for k in range(KD if "ev" not in skip else 0):
                    evm = cfg.get("evmode", "mix")
                    if (k % 2 == 0 and evm == "mix") or evm == "act":
                        ACT(h[:, k, j * 128:(j + 1) * 128], ptv[:, k, :], AF.Identity, [(BK[pb_], k), (ABF, "A"), (ABF, "B")], [(h, (k, j))],
                            bias=ABF[:, 1, k, vec:vec + 1], scale=ABF[:, 0, k, vec:vec + 1])
                    else:
                        TS("dve", h[:, k, j * 128:(j + 1) * 128], ptv[:, k, :], ABF[:, 0, k, vec:vec + 1], ABF[:, 1, k, vec:vec + 1], ALU.mult, ALU.add,
                           [(BK[pb_], k), (ABF, "A"), (ABF, "B")], [(h, (k, j))])

        def proj(gi):
            tiles, isctx = groups[gi]
            N = 128 * len(tiles)
            t0 = tiles[0] * 128
            h = hbuf[gi]
            if not isctx:
                LD(cosT[:, 0:N], cst["ropec"][:, t0 - CTX:t0 - CTX + N], [cst["ropec"]], [cosT])
                LD(sinT[:, 0:N], cst["ropes"][:, t0 - CTX:t0 - CTX + N], [cst["ropes"]], [sinT])
            chunks = []
            for i in range(4):
                chunks.append((i * 128, 128, "rope", PF_RQ + i * 128))
            for i in range(14):
                chunks.append((C_RW + i * 128, 128, "fm", PF_RW + i * 128))
            for i in range(4):
                chunks.append((C_GQ + i * 128, 128, "fm", PF_GQ + i * 128))
            if "ad" not in skip:
                chunks.append((C_GAD - 96, 128, "ad", 0))
            for i in range(12):
                chunks.append((C_GATE + i * 128, 128, "fm", PF_GATE + i * 128))
            for (c0, cw, kind, dst) in (chunks if "fm" not in skip else []):
                b = psA.next()
                ps = bkf(b)
                for k in range(KD):
                    MM(ps[0:cw, 0:N], WIN[:, k, c0:c0 + cw], h[:, k, 0:N], [WIN, h], [BK[b]], start=(k == 0), stop=(k == KD - 1))
                keys = [kPF(s, tt, dst // 128) for tt in tiles]
                if kind == "ad":
                    CP("dve", stf[96:128, 0:N], ps[96:128, 0:N], [BK[b]], [stf])
                    STO(ADG[s, :, t0:t0 + N], stf[96:128, 0:N], [stf], [(ADG, (s, tt)) for tt in tiles])
                    continue
                sg = stg.next()
                if kind == "rope" and not isctx and "rope" not in skip:
                    sb2 = stg.next()
                    CP("act", sb2[:, 0:N], ps[:, 0:N], [BK[b]], [sb2])
                    ps2 = bkf(6)
                    MM(ps2[:, 0:N], RMB[:, :], sb2[:, 0:N], [RMB, sb2], [BK[6]])
                    TT("dve", rt1[:, 0:N], ps[:, 0:N], cosT[:, 0:N], ALU.mult, [BK[b], cosT], [rt1])
                    TT("dve", rt2[:, 0:N], ps2[:, 0:N], sinT[:, 0:N], ALU.mult, [BK[6], sinT], [rt2])
                    TT("dve", sg[:, 0:N], rt1[:, 0:N], rt2[:, 0:N], ALU.add, [rt1, rt2], [sg])
                else:
                    evi[0] += 1
                    CP("act" if evi[0] % 2 else "dve", sg[:, 0:N], ps[:, 0:N], [BK[b]], [sg])
                STO(PF[s, dst:dst + cw, t0:t0 + N], sg[0:cw, 0:N], [sg], keys)
            for j, tt in enumerate(tiles if "tm" not in skip else []):
                for vi, vc in enumerate((C_RV, C_GV)):
                    b = psA.next()
                    ps = bkf(b)
                    for k in range(KD):
                        MM(ps[:, :], h[:, k, j * 128:(j + 1) * 128], WIN[:, k, vc:vc + 512], [WIN, h], [BK[b]], start=(k == 0), stop=(k == KD - 1))
                    sg = stg.next()
                    evi[0] += 1
                    CP("act" if evi[0] % 2 else "dve", sg[:, :], ps[:, :], [BK[b]], [sg])
                    STO(VT[s, tt * 128:(tt + 1) * 128, vi * 512:(vi + 1) * 512], sg[:, :], [sg], [(VT, (s, tt, vi))])


        prepH(0)
        for gi in range(len(groups)):
            if gi + 1 < len(groups):
                prepH(gi + 1)
            proj(gi)
    def stageG(l, s, d, ar, B):
        ar.reset([BK[b_] for b_ in B])
        SG = ar.alloc("SG", [128, 4, 128], F32)
        SGb = ar.alloc("SGb", [128, 4, 128], BF16)
        ADl = ar.alloc("ADl", [128, 128], F32)
        e1 = ar.alloc("e1", [128, 256], F32)
        sp = ar.alloc("sp", [128, 256], F32)
        NBUF = 2
        Kt = Ring([ar.alloc("Kt%d" % i, [128, 128], BF16) for i in range(NBUF)])
        QP = Ring([ar.alloc("QP%d" % i, [128, 2, 128], BF16) for i in range(NBUF)])
        Vt = Ring([ar.alloc("Vt%d" % i, [128, 256], BF16) for i in range(NBUF)])
        EBt = Ring([ar.alloc("EBt%d" % i, [128, 128], F32) for i in range(NBUF)])
        ENBt = Ring([ar.alloc("ENBt%d" % i, [128, 128], F32) for i in range(NBUF)])
        QT = Ring([ar.alloc("QT%d" % i, [128, 2, 128], BF16) for i in range(NBUF)])
        KT = Ring([ar.alloc("KT%d" % i, [128, 128], BF16) for i in range(NBUF)])
        KH = Ring([ar.alloc("KH%d" % i, [128, 128], BF16) for i in range(NBUF)])
        KHtm = Ring([ar.alloc("KHtm%d" % i, [128, 128], BF16) for i in range(NBUF)])
        STt = Ring([ar.alloc("STt%d" % i, [128, 256], BF16) for i in range(NBUF)])
        OUTt = Ring([ar.alloc("OUTt%d" % i, [128, 256], BF16) for i in range(NBUF)])
        MS("pool", SG[:, :, :], 0.0, [SG])
        MS("pool", SGb[:, :, :], 0.0, [SGb])
        MS("dve", ADl[:, :], 0.0, [ADl])
        MS("dve", ADl[32:33, :], 1.0, [ADl])
        for qp in QP.t:
            MS("pool", qp[:, :, :], 0.0, [qp])
        last = 127 if d == 0 else 0
        for tt in order(d):
            t0 = tt * 128
            LD(ADl[0:16, :], ADG[s, d * 16:(d + 1) * 16, t0:t0 + 128], [(ADG, (s, tt))], [ADl])
            MM(bkf(B[0])[:, 0:256], ADl[:, :], AUPG[:, d, :], [ADl, AUPG], [BK[B[0]]])
            ACT(e1[:, :], bkf(B[0])[:, 0:256], AF.Exp, [BK[B[0]]], [e1], scale=-1.0)
            ACT(sp[:, :], e1[:, :], AF.Ln, [e1, ONES], [sp], bias=ONES[:, 0:1])
            for hp in range(4):
                kt, qp, vt = Kt.next(), QP.next(), Vt.next()
                if hp < 2:
                    qr, kr, vc, oc = PF_RQ + hp * 128, PF_RK + hp * 128, hp * 256, hp * 256
                else:
                    qr, kr, vc, oc = PF_GQ + (hp - 2) * 128, PF_GK + (hp - 2) * 128, 512 + (hp - 2) * 256, 1024 + (hp - 2) * 256
                LD(kt[:, :], PF[s, kr:kr + 128, t0:t0 + 128], [kPF(s, tt, kr // 128)], [kt])
                LD(qp[0:64, 0, :], PF[s, qr:qr + 64, t0:t0 + 128], [kPF(s, tt, qr // 128)], [qp])
                LD(qp[64:128, 1, :], PF[s, qr + 64:qr + 128, t0:t0 + 128], [kPF(s, tt, qr // 128)], [qp])
                LD(vt[:, :], VT[s, t0:t0 + 128, vc:vc + 256], [(VT, (s, tt, vc // 512))], [vt])
                if hp < 2:
                    ebt, enbt = REB, RENB
                    eb = REB[:, d, hp, :]
                    enb = RENB[:, d, hp, :]
                    ebR = [(REB, d * 2 + hp)]
                    enbR = [(RENB, d * 2 + hp)]
                else:
                    ebt, enbt = EBt.next(), ENBt.next()
                    MM(bkf(B[0])[:, 256:384], sp[:, (hp - 2) * 128:(hp - 1) * 128], TRIG[:, d, :], [sp, TRIG], [BK[B[0]]])
                    ACT(ebt[:, :], bkf(B[0])[:, 256:384], AF.Exp, [BK[B[0]]], [ebt])
                    ACT(enbt[:, :], bkf(B[0])[:, 256:384], AF.Exp, [BK[B[0]]], [enbt], scale=-1.0)
                    eb, enb = ebt[:, :], enbt[:, :]
                    ebR, enbR = [ebt], [enbt]
                qT, kT, kH, kHtm, sT, oT = QT.next(), KT.next(), KH.next(), KHtm.next(), STt.next(), OUTt.next()
                ebb = eb.rearrange("p (o t) -> p o t", o=1).broadcast_to([128, 2, 128])
                STT(qT[:, :, :], qp[:, :, :], 0.125, ebb, ALU.mult, ALU.mult, [qp] + ebR, [qT])
                TT("pool", kT[:, :], kt[:, :], enb, ALU.mult, [kt] + enbR, [kT])
                TS("dve", kH[:, :], kT[:, :], eb[:, last:last + 1], None, ALU.mult, None, [kT] + ebR, [kH])
                TRP(bkb(B[1])[:, 0:128], kH[:, :], [kH], [BK[B[1]]])
                CP("act", kHtm[:, :], bkb(B[1])[:, 0:128], [BK[B[1]]], [kHtm])
                for h in range(2):
                    MM(bkf(B[2])[:, h * 128:(h + 1) * 128], kT[:, :], qT[:, h, :], [kT, qT], [(BK[B[2]], h)])
                TT("dve", sT[:, :], bkf(B[2])[:, 0:256], MASKG[:, d, :], ALU.mult, [BK[B[2]], MASKG], [sT])
                for h in range(2):
                    MM(bkf(B[3])[:, h * 128:(h + 1) * 128], sT[:, h * 128:(h + 1) * 128], vt[:, h * 128:(h + 1) * 128], [sT, vt], [(BK[B[3]], h)], start=True, stop=False)
                    MM(bkf(B[3])[:, h * 128:(h + 1) * 128], qT[:, h, :], SGb[:, hp, :], [qT, (SGb, hp)], [(BK[B[3]], h)], start=False, stop=True)
                CP("act", oT[:, :], bkf(B[3])[:, 0:256], [BK[B[3]]], [oT])
                STO(OD[d][s, t0:t0 + 128, oc:oc + 256], oT[:, :], [oT], [(OD[d], (s, tt, oc // 256))], eng="act")
                MM(bkf(B[1])[:, 128:384], kHtm[:, :], vt[:, :], [kHtm, vt], [BK[B[1]]])
                for h in range(2):
                    r0, r1 = h * 64, (h + 1) * 64
                    STT(SG[r0:r1, hp, :], SG[r0:r1, hp, :], eb[r0:r1, last:last + 1], bkf(B[1])[r0:r1, 128 + h * 128:128 + (h + 1) * 128], ALU.mult, ALU.add,
                        [(SG, hp), BK[B[1]]] + ebR, [(SG, hp)])
                CP("act", SGb[:, hp, :], SG[:, hp, :], [(SG, hp)], [(SGb, hp)])
                yield

    def stageRP(l, s, ar, B):
        ar.reset([BK[b_] for b_ in B])
        raw = Ring([ar.alloc("raw%d" % i, [128, 516], BF16) for i in range(3)])
        t1 = Ring([ar.alloc("t1_%d" % i, [128, 512], F32) for i in range(2)])
        t2 = Ring([ar.alloc("t2_%d" % i, [128, 512], F32) for i in range(2)])
        shb = Ring([ar.alloc("shb%d" % i, [128, 512], BF16) for i in range(3)])
        shf = Ring([ar.alloc("shf%d" % i, [128, 512], F32) for i in range(2)])
        kx = ar.alloc("kx", [128, 512], F32)
        sq = ar.alloc("sq", [128, 512], BF16)
        rn = ar.alloc("rn", [128, 512], F32)
        vsh = ar.alloc("vsh", [128, 4, 512], BF16)
        vts = Ring([ar.alloc("vts%d" % i, [128, 512], BF16) for i in range(2)])
        for gi, (tiles, isctx) in enumerate(groups):
            N = 128 * len(tiles)
            t0 = tiles[0] * 128
            lo, hi = (0, CTX) if isctx else (CTX, T)
            a0 = max(t0 - 1, lo)
            a1 = min(t0 + N + 1, hi)
            nb_t = [tt for tt in (tiles[0] - 1, tiles[-1] + 1) if lo <= tt * 128 < hi]
            for ci in range(14):
                rw = raw.next()
                if a0 == t0:
                    MS("dve", rw[:, 0:2], 0.0, [rw])
                if a1 == t0 + N:
                    MS("dve", rw[:, N:N + 2], 0.0, [rw])
                off = 1 - (t0 - a0)
                LD(rw[:, off:off + (a1 - a0)], PF[s, PF_RW + ci * 128:PF_RW + (ci + 1) * 128, a0:a1],
                   [kPF(s, tt, (PF_RW // 128) + ci) for tt in tiles + nb_t], [rw])
                u1, u2 = t1.next(), t2.next()
                ACT(u1[:, 0:N], rw[:, 1:N + 1], AF.Identity, [rw, (TAB2, "c0")], [u1], scale=TAB2[:, ci:ci + 1])
                STT(u2[:, 0:N], rw[:, 0:N], TAB[:, 24 + ci:25 + ci], u1[:, 0:N], ALU.mult, ALU.add, [rw, TAB, u1], [u2])
                if ci < 4 or ci == 13:
                    ob = shb.next()
                    STT(ob[:, 0:N], rw[:, 2:N + 2], TAB[:, 38 + ci:39 + ci], u2[:, 0:N], ALU.mult, ALU.add, [rw, TAB, u2], [ob])
                    row = ci * 128 if ci < 4 else 1536
                    STO(RW[s, row:row + 128, t0:t0 + N], ob[:, 0:N], [ob], [(RW, (s, tt, row // 128)) for tt in tiles])
                elif ci < 8:
                    hp = ci - 4
                    of = shf.next()
                    STT(of[:, 0:N], rw[:, 2:N + 2], TAB[:, 38 + ci:39 + ci], u2[:, 0:N], ALU.mult, ALU.add, [rw, TAB, u2], [of])
                    ob = shb.next()
                    CP("act", ob[:, 0:N], of[:, 0:N], [of], [ob])
                    STO(RW[s, 512 + hp * 128:512 + (hp + 1) * 128, t0:t0 + N], ob[:, 0:N], [ob], [(RW, (s, tt, 4 + hp)) for tt in tiles])
                    ACT(kx[:, 0:N], of[:, 0:N], AF.Identity, [of, TAB], [kx], scale=TAB[:, 52 + hp:53 + hp])
                    ACT(sq[:, 0:N], kx[:, 0:N], AF.Square, [kx], [sq])
                    MM(bkf(B[0])[:, 0:N], BLK1[:, :], sq[:, 0:N], [BLK1, sq], [BK[B[0]]])
                    ACT(rn[:, 0:N], bkf(B[0])[:, 0:N], AF.Ln, [BK[B[0]]], [rn])
                    ACT(rn[:, 0:N], rn[:, 0:N], AF.Exp, [rn], [rn], scale=-0.5)
                    ob2 = shb.next()
                    TT("dve", ob2[:, 0:N], kx[:, 0:N], rn[:, 0:N], ALU.mult, [kx, rn], [ob2])
                    STO(RW[s, 1024 + hp * 128:1024 + (hp + 1) * 128, t0:t0 + N], ob2[:, 0:N], [ob2], [(RW, (s, tt, 8 + hp)) for tt in tiles])
                elif ci < 12:
                    hp = ci - 8
                    STT(vsh[:, hp, 0:N], rw[:, 2:N + 2], TAB[:, 38 + ci:39 + ci], u2[:, 0:N], ALU.mult, ALU.add, [rw, TAB, u2], [(vsh, hp)])
                else:
                    of = shf.next()
                    STT(of[:, 0:N], rw[:, 2:N + 2], TAB[:, 38 + ci:39 + ci], u2[:, 0:N], ALU.mult, ALU.add, [rw, TAB, u2], [of])
                    ACT(of[:, 0:N], of[:, 0:N], AF.Tanh, [of], [of])
                    STO(RWF[s, :, t0:t0 + N], of[:, 0:N], [of], [(RWF, (s, tt)) for tt in tiles])
                yield
            for j, tt in enumerate(tiles):
                pb = B[1 + (j % 2)]
                for hp in range(4):
                    TRP(bkb(pb)[:, hp * 128:(hp + 1) * 128], vsh[:, hp, j * 128:(j + 1) * 128], [(vsh, hp)], [(BK[pb], hp)])
                vo = vts.next()
                CP("act", vo[:, :], bkb(pb)[:, 0:512], [BK[pb]], [vo])
                STO(VT[s, tt * 128:(tt + 1) * 128, 1024:1536], vo[:, :], [vo], [(VT, (s, tt, 2))])

    def stageR(l, s, d, ar, B):
        ar.reset([BK[b_] for b_ in B])
        STs = ar.alloc("STs", [128, 4, 64], F32)
        STb = ar.alloc("STb", [128, 4, 64], BF16)
        TWl = ar.alloc("TWl", [128, 128], F32)
        ADr = Ring([ar.alloc("ADr%d" % i, [128, 128], BF16) for i in range(2)])
        sig = ar.alloc("sig", [128, 512], F32)
        RK = Ring([ar.alloc("RK%d" % i, [128, 3, 128], BF16) for i in range(2)])
        E4 = Ring([ar.alloc("E4_%d" % i, [128, 4, 384], F32) for i in range(1)])
        PCT = Ring([ar.alloc("PCT%d" % i, [128, 2, 4, 2], F32) for i in range(3)])
        En = Ring([ar.alloc("En%d" % i, [128, 128], F32) for i in range(2)])
        av = Ring([ar.alloc("av%d" % i, [128, 128], F32) for i in range(4)])
        tmpv = Ring([ar.alloc("tmpv%d" % i, [128, 128], F32) for i in range(2)])
        kdv = Ring([ar.alloc("kdv%d" % i, [128, 128], F32) for i in range(2)])
        kav = Ring([ar.alloc("kav%d" % i, [128, 128], F32) for i in range(2)])
        prv = Ring([ar.alloc("prv%d" % i, [128, 128], BF16) for i in range(2)])
        cft = Ring([ar.alloc("cft%d" % i, [128, 8], F32) for i in range(2)])
        NPB = 2
        APAD = Ring([ar.alloc("APAD%d" % i, [128, 4, 2, 2, 2, 64], BF16) for i in range(NPB)])
        BPAD = Ring([ar.alloc("BPAD%d" % i, [128, 4, 2, 2, 64], BF16) for i in range(NPB)])
        KPAD = Ring([ar.alloc("KPAD%d" % i, [128, 4, 2, 2, 64], BF16) for i in range(NPB)])
        KHP = Ring([ar.alloc("KHP%d" % i, [128, 4, 2, 2, 64], BF16) for i in range(NPB)])
        BHP = Ring([ar.alloc("BHP%d" % i, [128, 4, 2, 2, 64], BF16) for i in range(NPB)])
        VST = Ring([ar.alloc("VST%d" % i, [128, 4, 64], BF16) for i in range(3)])
        Xm = [ar.alloc("Xm%d" % i, [128, 4, 128], BF16) for i in range(2)]
        Ym = [ar.alloc("Ym%d" % i, [128, 4, 128], BF16) for i in range(2)]
        Hm = [ar.alloc("Hm%d" % i, [128, 4, 128], BF16) for i in range(2)]
        HF = [ar.alloc("HF%d" % i, [128, 4, 128], BF16) for i in range(2)]
        MAKs = [ar.alloc("MAK%d" % i, [128, 4, 128], BF16) for i in range(2)]
        ARKs = [ar.alloc("ARK%d" % i, [128, 4, 128], BF16) for i in range(2)]
        ARBs = [ar.alloc("ARB%d" % i, [128, 4, 128], BF16) for i in range(2)]
        Xs = ar.alloc("Xs", [128, 4, 64], BF16)
        Us = ar.alloc("Us", [128, 4, 64], BF16)
        Oo = Ring([ar.alloc("Oo%d" % i, [128, 4, 64], BF16) for i in range(2)])
        KBt = ar.alloc("KBt", [128, 4, 2, 128], BF16)
        MS("pool", STs[:, :, :], 0.0, [STs])
        MS("pool", STb[:, :, :], 0.0, [STb])
        MS("dve", TWl[:, :], 1.0, [TWl])
        for rg in (APAD, BPAD, KPAD, KHP, BHP):
            for tl in rg.t:
                full = tuple(slice(None) for _ in tl.h.shape)
                MS("pool", tl[full], 0.0, [tl])
        lastc = 63 if d == 0 else 0
        idbb = IDB[:, :].rearrange("p (o t) -> p o t", o=1).broadcast_to([128, 4, 128])
        bP, bX, bY, bH, bS = B[2], B[0], B[1], B[2], B[3]

        def mkb(i):
            return MK[:, d, i, :].rearrange("p (o t) -> p o t", o=1).broadcast_to([128, 2, 128])

        def prep(tt, out):
            t0 = tt * 128
            LD(TWl[0:64, :], RWF[s, d * 64:(d + 1) * 64, t0:t0 + 128], [(RWF, (s, tt))], [TWl])
            MM(bkf(bP)[:, :], TWl[0:65, :], WUP[0:65, d, :], [TWl, WUP], [BK[bP]])
            ACT(sig[:, :], bkf(bP)[:, :], AF.Sigmoid, [BK[bP]], [sig])
            adl = ADr.next()
            LD(adl[:, :], RW[s, 1536:1664, t0:t0 + 128], [(RW, (s, tt, 12))], [adl])
            ap_, bp_, kp_, khp_, bhp_ = APAD.next(), BPAD.next(), KPAD.next(), KHP.next(), BHP.next()
            e4 = E4.next()
            pct = PCT.next()
            cf = cft.next()
            out.update(ap=ap_, bp=bp_, kp=kp_, khp=khp_, bhp=bhp_, pct=pct)
            a_list = []
            for hp in range(4):
                MM(bkf(bP)[:, 384:512], AUP[:, d, hp * 128:(hp + 1) * 128], adl[:, :], [AUP, adl], [BK[bP]])
                a_ = av.next()
                ACT(a_[:, :], bkf(bP)[:, 384:512], AF.Sigmoid, [BK[bP], TAB], [a_], bias=TAB[:, 64 + d * 4 + hp:65 + d * 4 + hp])
                a_list.append(a_)
            yield
            for hp in range(4):
                rk = RK.next()
                LD(rk[:, :, :], RW[s, 0:1536, t0:t0 + 128].rearrange("(j q) t -> q j t", q=512)[hp * 128:(hp + 1) * 128, :, :],
                   [(RW, (s, tt, hp)), (RW, (s, tt, 4 + hp)), (RW, (s, tt, 8 + hp))], [rk])
                MM(bkf(bP)[:, 0:384], sig[:, hp * 128:(hp + 1) * 128], TRIR[:, d, :], [sig, TRIR], [BK[bP]])
                ACT(e4[:, hp, :], bkf(bP)[:, 0:384], AF.Exp, [BK[bP]], [(e4, hp)])
                en = En.next()
                ACT(en[:, :], bkf(bP)[:, 0:128], AF.Exp, [BK[bP]], [en], scale=-1.0)
                for c in range(2):
                    col = c * 64 + lastc
                    CP("pool", pct[:, c, hp, 0:1], e4[:, hp, col:col + 1], [(e4, hp)], [(pct, (c, hp))])
                a_ = a_list[hp]
                tm_, kd_, ka_, pr_ = tmpv.next(), kdv.next(), kav.next(), prv.next()
                TS("pool", tm_[:, :], a_[:, :], TAB[:, 56 + hp:57 + hp], TAB2[:, 14 + hp:15 + hp], ALU.mult, ALU.add, [a_, TAB, (TAB2, "omka")], [tm_])
                TT("dve", kd_[:, :], rk[:, 1, :], tm_[:, :], ALU.mult, [rk, tm_], [kd_])
                TT("pool", ka_[:, :], rk[:, 2, :], a_[:, :], ALU.mult, [rk, a_], [ka_])
                STT(pr_[:, :], rk[:, 0, :], TAB[:, 60 + hp:61 + hp], kd_[:, :], ALU.mult, ALU.mult, [rk, TAB, kd_], [pr_])
                MM(bkf(bP)[:, 384 + hp * 2:386 + hp * 2], pr_[:, :], HSEL[:, :], [pr_, HSEL], [BK[bP]])
                CP("act", cf[:, hp * 2:hp * 2 + 2], bkf(bP)[:, 384 + hp * 2:386 + hp * 2], [BK[bP]], [(cf, hp)])

                def v3(ap):
                    return ap.rearrange("p (c t) -> p c t", c=2)
                for sl in range(2):
                    r0, r1 = sl * 64, (sl + 1) * 64
                    TT("dve", ap_[r0:r1, hp, :, 0, sl, :], v3(rk[r0:r1, 2, :]), v3(e4[r0:r1, hp, 128:256]), ALU.mult, [rk, (e4, hp)], [(ap_, hp)])
                    TT("dve", ap_[r0:r1, hp, :, 1, sl, :], v3(rk[r0:r1, 0, :]), v3(e4[r0:r1, hp, 0:128]), ALU.mult, [rk, (e4, hp)], [(ap_, hp)])
                    TT("pool", bp_[r0:r1, hp, :, sl, :], v3(ka_[r0:r1, :]), v3(en[r0:r1, :]), ALU.mult, [ka_, en], [(bp_, hp)])
                    TT("pool", kp_[r0:r1, hp, :, sl, :], v3(kd_[r0:r1, :]), v3(en[r0:r1, :]), ALU.mult, [kd_, en], [(kp_, hp)])
                    TT("pool", khp_[r0:r1, hp, :, sl, :], v3(kd_[r0:r1, :]), v3(e4[r0:r1, hp, 256:384]), ALU.mult, [kd_, (e4, hp)], [(khp_, hp)])
                    STT(bhp_[r0:r1, hp, :, sl, :], v3(ka_[r0:r1, :]), -1.0, v3(e4[r0:r1, hp, 256:384]), ALU.mult, ALU.mult, [ka_, (e4, hp)], [(bhp_, hp)])
                yield
            STO(CF[d][s, t0:t0 + 128, :], cf[:, :], [cf], [(CF[d], (s, tt))], eng="act")

        def GI(tt, c, bs, tb, out):
            ap_, bp_, kp_ = tb["ap"], tb["bp"], tb["kp"]
            tc0 = tt * 128 + c * 64
            vst = VST.next()
            out["vst"] = vst
            vsrc = VT[s, tc0:tc0 + 64, 1024:1536].rearrange("t (h s v) -> t h s v", h=4, s=2)
            for sl in range(2):
                LD(vst[sl * 64:(sl + 1) * 64, :, :], vsrc[:, :, sl, :], [(VT, (s, tt, 2))], [vst])
            MAK, ARK, ARB = MAKs[bs], ARKs[bs], ARBs[bs]
            for hp in range(4):
                MM(bkf(bH)[:, hp * 128:(hp + 1) * 128], ap_[:, hp, c, 0, :, :], bp_[:, hp, c, :, :], [(ap_, hp), (bp_, hp)], [BK[bH]])
            X, Y, H = Xm[0], Ym[0], Hm[0]
            TT("dve", Y[:, :, :], bkf(bH).rearrange("p (h t) -> p h t", h=4), MK[:, d, 4, :].rearrange("p (o t) -> p o t", o=1).broadcast_to([128, 4, 128]),
               ALU.mult, [BK[bH], MK], [Y])
            for half in range(2):
                for hp in (2 * half, 2 * half + 1):
                    o_ = (hp % 2) * 256
                    MM(bkf(bX)[:, o_:o_ + 256], bp_[:, hp, c, :, :], ap_[:, hp, c, :, :, :], [(bp_, hp), (ap_, hp)], [BK[bX]])
                    MM(bkf(bY)[:, o_:o_ + 256], kp_[:, hp, c, :, :], ap_[:, hp, c, :, :, :], [(kp_, hp), (ap_, hp)], [BK[bY]])
                g1 = bkf(bX).rearrange("p (h x t) -> p h x t", h=2, x=2)
                g2 = bkf(bY).rearrange("p (h x t) -> p h x t", h=2, x=2)
                hs_ = slice(half * 2, half * 2 + 2)
                TT("dve", X[:, hs_, :], g1[:, :, 0, :], mkb(0), ALU.mult, [BK[bX], MK], [(X, half)])
                TT("dve", ARB[:, hs_, :], g1[:, :, 1, :], mkb(1), ALU.mult, [BK[bX], MK], [(ARB, half)])
                TT("dve", MAK[:, hs_, :], g2[:, :, 0, :], mkb(2), ALU.mult, [BK[bY], MK], [(MAK, half)])
                TT("dve", ARK[:, hs_, :], g2[:, :, 1, :], mkb(3), ALU.mult, [BK[bY], MK], [(ARK, half)])
                yield
            TT("pool", H[:, :, :], X[:, :, :], idbb, ALU.add, [X, IDB], [H])
            for k in range(1, 6):
                Xn, Yn = Xm[k % 2], Ym[k % 2]
                Hn = Hm[k % 2] if k < 5 else HF[bs]
                if k <= 4:
                    for hp in range(4):
                        MM(bkf(bX)[:, hp * 128:(hp + 1) * 128], Y[:, hp, :], X[:, hp, :], [Y, X], [BK[bX]])
                for hp in range(4):
                    MM(bkf(bY)[:, hp * 128:(hp + 1) * 128], X[:, hp, :], Y[:, hp, :], [Y, X], [BK[bY]])
                if k <= 4:
                    CP("act", Xn[:, :, :], bkf(bX).rearrange("p (h t) -> p h t", h=4), [BK[bX]], [Xn])
                CP("act", Yn[:, :, :], bkf(bY).rearrange("p (h t) -> p h t", h=4), [BK[bY]], [Yn])
                yield
                for hp in range(4):
                    MM(bkf(bH)[:, hp * 128:(hp + 1) * 128], Yn[:, hp, :], H[:, hp, :], [Yn, H], [BK[bH]])
                TT("dve", Hn[:, :, :], bkf(bH).rearrange("p (h t) -> p h t", h=4), H[:, :, :], ALU.add, [BK[bH], H], [Hn])
                yield
                X, Y, H = Xn, Yn, Hn

        def ST(tt, c, bs, tb, cb):
            ap_, khp_, bhp_, pct = tb["ap"], tb["khp"], tb["bhp"], tb["pct"]
            vst = cb["vst"]
            MAK, ARK, ARB, H = MAKs[bs], ARKs[bs], ARBs[bs], HF[bs]
            tc0 = tt * 128 + c * 64
            for hp in range(4):
                MM(bkf(bS)[:, hp * 64:(hp + 1) * 64], ap_[:, hp, c, 0, :, :], STb[:, hp, :], [(ap_, hp), STb], [BK[bS]], start=True, stop=False)
                MM(bkf(bS)[:, hp * 64:(hp + 1) * 64], MAK[:, hp, :], vst[:, hp, :], [MAK, vst], [BK[bS]], start=False, stop=True)
            CP("act", Xs[:, :, :], bkf(bS)[:, 0:256].rearrange("p (h v) -> p h v", h=4), [BK[bS]], [Xs])
            yield
            for hp in range(4):
                MM(bkf(bS)[:, hp * 64:(hp + 1) * 64], H[:, hp, :], Xs[:, hp, :], [H, Xs], [BK[bS]])
            CP("act", Us[:, :, :], bkf(bS)[:, 0:256].rearrange("p (h v) -> p h v", h=4), [BK[bS]], [Us])
            yield
            for hp in range(4):
                MM(bkf(bS)[:, hp * 64:(hp + 1) * 64], ap_[:, hp, c, 1, :, :], STb[:, hp, :], [(ap_, hp), STb], [BK[bS]], start=True, stop=False)
                MM(bkf(bS)[:, hp * 64:(hp + 1) * 64], ARK[:, hp, :], vst[:, hp, :], [ARK, vst], [BK[bS]], start=False, stop=False)
                MM(bkf(bS)[:, hp * 64:(hp + 1) * 64], ARB[:, hp, :], Us[:, hp, :], [ARB, Us], [BK[bS]], start=False, stop=True)
            oo = Oo.next()
            CP("act", oo[:, :, :], bkf(bS)[:, 0:256].rearrange("p (h v) -> p h v", h=4), [BK[bS]], [oo])
            odst = OD[d][s, tc0:tc0 + 64, 512:1024].rearrange("t (h s v) -> t h s v", h=4, s=2)
            for sl in range(2):
                STO(odst[:, :, sl, :], oo[sl * 64:(sl + 1) * 64, :, :], [oo], [(OD[d], (s, tt, 2 + c))], eng="act")
            yield
            kbv = bkb(bS).rearrange("p (h x t) -> p h x t", h=4, x=2)
            for hp in range(4):
                TRP(kbv[:, hp, 0, :], khp_[:, hp, c, :, :], [(khp_, hp)], [BK[bS]])
                TRP(kbv[:, hp, 1, :], bhp_[:, hp, c, :, :], [(bhp_, hp)], [BK[bS]])
            CP("act", KBt[:, :, :, :], kbv, [BK[bS]], [KBt])
            yield
            for hp in range(4):
                MM(bkf(bS)[:, hp * 64:(hp + 1) * 64], KBt[:, hp, 0, :], vst[:, hp, :], [KBt, vst], [BK[bS]], start=True, stop=False)
                MM(bkf(bS)[:, hp * 64:(hp + 1) * 64], KBt[:, hp, 1, :], Us[:, hp, :], [KBt, Us], [BK[bS]], start=False, stop=True)
            pcb = pct[:, c, :, 0:1].broadcast_to([128, 4, 64])
            TT("dve", STs[:, :, :], STs[:, :, :], pcb, ALU.mult, [STs, pct], [STs])
            TT("dve", STs[:, :, :], bkf(bS)[:, 0:256].rearrange("p (h v) -> p h v", h=4), STs[:, :, :], ALU.add, [BK[bS], STs], [STs])
            CP("act", STb[:, :, :], STs[:, :, :], [STs], [STb])
            yield

        seq = []
        for tt in order(d):
            for c in ((0, 1) if d == 0 else (1, 0)):
                seq.append((tt, c))
        tbufs = {}

        def front(q):
            tt, c = seq[q]
            if tt not in tbufs:
                tbufs[tt] = {}
                yield from prep(tt, tbufs[tt])
            cbufs[q] = {}
            yield from GI(tt, c, q % 2, tbufs[tt], cbufs[q])

        cbufs = {}
        for _ in front(0):
            yield
        for q in range(len(seq)):
            tt, c = seq[q]
            gens = [ST(tt, c, q % 2, tbufs[tt], cbufs[q])]
            if q + 1 < len(seq):
                gens.append(front(q + 1))
            while gens:
                for g in list(gens):
                    try:
                        next(g)
                    except StopIteration:
                        gens.remove(g)
                yield

    def stageC(l, s, ar, B):
        ar.reset([BK[b_] for b_ in B])
        last_layer = (l == L - 1)
        GROWL = ar.alloc("GROWL", [128, D], F32)
        LD(GROWL[:, :].rearrange("p (o n) -> p o n", o=1), GD[l, s:s + 1, :].partition_broadcast(128), [(GD, (l, s))], [GROWL])
        GROWC = ar.alloc("GROWC", [128, D], F32)
        LD(GROWC[:, :].rearrange("p (o n) -> p o n", o=1), GD[l, NB:NB + 1, :].partition_broadcast(128), [(GD, (l, NB))], [GROWC])
        Of = Ring([ar.alloc("Of%d" % i, [128, 1536], BF16) for i in range(2)])
        Ob = Ring([ar.alloc("Ob%d" % i, [128, 1536], BF16) for i in range(2)])
        o32 = ar.alloc("o32", [128, 1536], F32)
        sq = ar.alloc("sqc", [128, 1536], F32)
        yb = ar.alloc("yb", [128, 1536], BF16)
        gT = Ring([ar.alloc("gT%d" % i, [128, 12, 128], BF16) for i in range(1)])
        sgT = ar.alloc("sgT", [128, 12, 128], BF16)
        yT = ar.alloc("yT", [128, 12, 128], BF16)
        vr = Ring([ar.alloc("vr%d" % i, [128, 512], BF16) for i in range(1)])
        cfa = Ring([ar.alloc("cfa%d" % i, [128, 8], F32) for i in range(2)])
        cfb = Ring([ar.alloc("cfb%d" % i, [128, 8], F32) for i in range(2)])
        xt = Ring([ar.alloc("xt%d" % i, [128, D], F32) for i in range(2)])
        xo = Ring([ar.alloc("xo%d" % i, [128, D], F32) for i in range(1)])
        stt_ = Ring([ar.alloc("stc%d" % i, [128, 80], F32) for i in range(2)])
        bon = ar.alloc("bon", [128, 512], F32)
        junkC = ar.alloc("junkC", [128, 512], BF16)
        for tt in range(NT):
            isctx = tt < CT
            if isctx and last_layer:
                continue
            t0 = tt * 128
            of_, ob_ = Of.next(), Ob.next()
            LD(of_[:, :], OD[0][s, t0:t0 + 128, :], [(OD[0], (s, tt, i)) for i in range(6)], [of_])
            LD(ob_[:, :], OD[1][s, t0:t0 + 128, :], [(OD[1], (s, tt, i)) for i in range(6)], [ob_])
            g_ = gT.next()
            LD(g_[:, :, :], PF[s, PF_GATE:PF_GATE + 1536, t0:t0 + 128].rearrange("(c p) t -> p c t", p=128),
               [kPF(s, tt, PF_GATE // 128 + i) for i in range(12)], [g_])
            v_ = vr.next()
            LD(v_[:, :], VT[s, t0:t0 + 128, 1024:1536], [(VT, (s, tt, 2))], [v_])
            ca, cb = cfa.next(), cfb.next()
            LD(ca[:, :], CF[0][s, t0:t0 + 128, :], [(CF[0], (s, tt))], [ca])
            LD(cb[:, :], CF[1][s, t0:t0 + 128, :], [(CF[1], (s, tt))], [cb])
            x_ = xt.next()
            if l == 0:
                if isctx:
                    LD(x_[:, :], ctx_in[s, t0:t0 + 128, :], [ctx_in], [x_])
                else:
                    LD(x_[:, :], x_in[s, t0 - CTX:t0 - CTX + 128, :], [x_in], [x_])
            else:
                LD(x_[:, :], xs[s, t0:t0 + 128, :], [(xs, (s, tt))], [x_])
            st = stt_.next()
            TT("dve", o32[:, :], of_[:, :], ob_[:, :], ALU.add, [of_, ob_], [o32])
            ACT(sq[:, :], o32[:, :], AF.Square, [o32], [sq])
            def red(dst, src, R, W):
                P.op("dve", lambda e: e.tensor_reduce(out=dst, in_=src, axis=AX.X, op=ALU.add), R, W)
            red(st[:, 0:4], o32[:, 0:512].rearrange("p (h e) -> p h e", h=4), [o32], [(st, "s1a")])
            red(st[:, 4:12], o32[:, 512:1024].rearrange("p (h e) -> p h e", h=8), [o32], [(st, "s1b")])
            red(st[:, 12:16], sq[:, 0:512].rearrange("p (h e) -> p h e", h=4), [sq], [(st, "s2a")])
            red(st[:, 16:24], sq[:, 512:1024].rearrange("p (h e) -> p h e", h=8), [sq], [(st, "s2b")])
            red(st[:, 24:28], sq[:, 1024:1536].rearrange("p (h e) -> p h e", h=4), [sq], [(st, "s2c")])
            TS("dve", st[:, 28:32], st[:, 0:4], 1.0 / 128, None, ALU.mult, None, [(st, "s1a")], [(st, "m")])
            TS("dve", st[:, 32:40], st[:, 4:12], 1.0 / 64, None, ALU.mult, None, [(st, "s1b"), (st, "m")], [(st, "m")])
            TS("dve", st[:, 40:44], st[:, 12:16], 1.0 / 128, None, ALU.mult, None, [(st, "s2a")], [(st, "e")])
            TS("dve", st[:, 44:52], st[:, 16:24], 1.0 / 64, None, ALU.mult, None, [(st, "s2b"), (st, "e")], [(st, "e")])
            TS("dve", st[:, 52:56], st[:, 24:28], 1.0 / 128, None, ALU.mult, None, [(st, "s2c"), (st, "e")], [(st, "e")])
            TT("dve", st[:, 56:60], st[:, 28:32], st[:, 28:32], ALU.mult, [(st, "m")], [(st, "mm")])
            TT("dve", st[:, 60:64], st[:, 32:36], st[:, 32:36], ALU.mult, [(st, "m"), (st, "mm")], [(st, "mm")])
            TT("dve", st[:, 40:44], st[:, 40:44], st[:, 56:60], ALU.subtract, [(st, "e"), (st, "mm")], [(st, "e")])
            TT("dve", st[:, 44:48], st[:, 44:48], st[:, 60:64], ALU.subtract, [(st, "e"), (st, "mm")], [(st, "e")])
            TT("dve", st[:, 56:60], st[:, 36:40], st[:, 36:40], ALU.mult, [(st, "m"), (st, "mm"), (st, "e")], [(st, "mm")])
            TT("dve", st[:, 48:52], st[:, 48:52], st[:, 56:60], ALU.subtract, [(st, "e"), (st, "mm")], [(st, "e")])
            ACT(st[:, 40:44], st[:, 40:44], AF.Sqrt, [(st, "e"), EPS], [(st, "e")], bias=EPS[:, 0:1])
            ACT(st[:, 44:52], st[:, 44:52], AF.Sqrt, [(st, "e"), EPS], [(st, "e")], bias=EPS[:, 1:2])
            ACT(st[:, 52:56], st[:, 52:56], AF.Sqrt, [(st, "e"), EPS], [(st, "e")], bias=EPS[:, 0:1])
            P.op("dve", lambda e, o=st[:, 40:56], i_=st[:, 40:56]: e.reciprocal(out=o, in_=i_), [(st, "e")], [(st, "e")])

            def bc(ap, h, e):
                return ap.rearrange("p (h o) -> p h o", o=1).broadcast_to([128, h, e])
            o3 = o32[:, 0:512].rearrange("p (h e) -> p h e", h=4)
            TT("dve", o3, o3, bc(st[:, 28:32], 4, 128), ALU.subtract, [o32, (st, "m")], [(o32, "r")])
            TT("dve", o3, o3, bc(st[:, 40:44], 4, 128), ALU.mult, [(o32, "r"), (st, "e")], [(o32, "r")])
            o3 = o32[:, 512:1024].rearrange("p (h e) -> p h e", h=8)
            TT("dve", o3, o3, bc(st[:, 32:40], 8, 64), ALU.subtract, [o32, (st, "m")], [(o32, "w")])
            TT("dve", o3, o3, bc(st[:, 44:52], 8, 64), ALU.mult, [(o32, "w"), (st, "e")], [(o32, "w")])
            o3 = o32[:, 1024:1536].rearrange("p (h e) -> p h e", h=4)
            TT("dve", o3, o3, bc(st[:, 52:56], 4, 128), ALU.mult, [o32, (st, "e")], [(o32, "g")])
            TT("dve", o32[:, :], o32[:, :], ROWS[:, 0:1536], ALU.mult, [o32, ROWS], [o32])
            TT("dve", o32[:, 512:1024], o32[:, 512:1024], ROWS[:, 1536:2048], ALU.add, [o32, ROWS], [(o32, "w")])
            TT("dve", ca[:, :], ca[:, :], cb[:, :], ALU.add, [ca, cb], [ca])
            TT("dve", bon[:, :].rearrange("p (h e) -> p h e", h=8), v_[:, :].rearrange("p (h e) -> p h e", h=8), bc(ca[:, 0:8], 8, 64), ALU.mult, [v_, ca], [bon])
            TT("dve", yb[:, 512:1024], o32[:, 512:1024], bon[:, :], ALU.add, [o32, bon], [(yb, 1)])
            CP("act", yb[:, 0:512], o32[:, 0:512], [o32], [(yb, 0)])
            CP("act", yb[:, 1024:1536], o32[:, 1024:1536], [o32], [(yb, 2)])
            ACT(sgT[:, :, :], g_[:, :, :], AF.Silu, [g_], [sgT])
            ytv = [bkb(B[0]).rearrange("p (c t) -> p c t", c=8), bkb(B[1]).rearrange("p (c t) -> p c t", c=8)]
            for c in range(12):
                TRP(ytv[c // 8][:, c % 8, :], yb[:, c * 128:(c + 1) * 128], [yb], [(BK[B[c // 8]], c % 8)])
            TT("dve", yT[:, 0:8, :], ytv[0][:, 0:8, :], sgT[:, 0:8, :], ALU.mult, [BK[B[0]], sgT], [(yT, 0)])
            TT("dve", yT[:, 8:12, :], ytv[1][:, 0:4, :], sgT[:, 8:12, :], ALU.mult, [BK[B[1]], sgT], [(yT, 1)])
            for n in range(2):
                for c in range(12):
                    MM(bkf(B[2 + n])[:, :], yT[:, c, :], WOUT[:, c, n * 512:(n + 1) * 512], [yT, WOUT], [BK[B[2 + n]]], start=(c == 0), stop=(c == 11))
            for n in range(2):
                ACT(junkC[:, :], bkf(B[2 + n])[:, :], AF.Square, [BK[B[2 + n]]], [junkC, (st, ("z", n))], accum=st[:, 68 + n:69 + n])
            TT("dve", st[:, 70:71], st[:, 68:69], st[:, 69:70], ALU.add, [(st, ("z", 0)), (st, ("z", 1))], [(st, "zz")])
            ACT(st[:, 71:72], st[:, 70:71], AF.Sqrt, [(st, "zz"), EPS], [(st, "zs")], bias=EPS[:, 0:1], scale=1.0 / D)
            P.op("dve", lambda e, o=st[:, 72:73], i_=st[:, 71:72]: e.reciprocal(out=o, in_=i_), [(st, "zs")], [(st, "zr")])
            xo_ = xo.next()
            grow = GROWC if isctx else GROWL
            for n in range(2):
                cs = slice(n * 512, (n + 1) * 512)
                STT(xo_[:, cs], bkf(B[2 + n])[:, :], st[:, 72:73], grow[:, cs], ALU.mult, ALU.mult, [BK[B[2 + n]], (st, "zr"), grow], [(xo_, n)])
                TT("pool", xo_[:, cs], xo_[:, cs], x_[:, cs], ALU.add, [(xo_, n), x_], [(xo_, n)])
            if last_layer:
                STO(y_out[s, t0 - CTX:t0 - CTX + 128, :], xo_[:, :], [xo_], [(y_out, (s, tt))], is_out=True)
            else:
                STO(xs[s, t0:t0 + 128, :], xo_[:, :], [xo_], [(xs, (s, tt))])
            yield

    stages = cfg.get("stages", "ABRC")

    def pipeline(l, s, ar, B):
        if "B" in stages:
            yield from stageG(l, s, 0, ar, B)
            yield from stageG(l, s, 1, ar, B)
        if "R" in stages:
            yield from stageRP(l, s, ar, B)
            yield from stageR(l, s, 0, ar, B)
            yield from stageR(l, s, 1, ar, B)
        if "C" in stages:
            yield from stageC(l, s, ar, B)

    for l in range(L):
        layer_setup(l)
        for s in range(NB):
            if "A" in stages:
                stageA(l, s)
        ilv = cfg.get("interleave", True)
        for s0 in range(0, NB, 2):
            if ilv and s0 + 1 < NB:
                gens = [pipeline(l, s0, AR, [0, 1, 2, 3]), pipeline(l, s0 + 1, AR2, [4, 5, 6, 7])]
            else:
                gens = [pipeline(l, s0, AR, [0, 1, 2, 3])]
                if s0 + 1 < NB:
                    gens.append(None)
            if len(gens) == 2 and gens[1] is None:
                for _ in gens[0]:
                    pass
                for _ in pipeline(l, s0 + 1, AR, [0, 1, 2, 3]):
                    pass
                continue
            while gens:
                for g in list(gens):
                    try:
                        next(g)
                    except StopIteration:
                        gens.remove(g)
    if not P.out_dmas:
        STO(y_out[0, 0:1, 0:8], JUNK[0:1, 0:8], [JUNK], [y_out], is_out=True)
    P.emit()
    return nc


def make_in_maps(inp, cfg, n_cores):
    SEQ, CTX, L, NB = cfg["SEQ"], cfg["CTX"], cfg["L"], cfg["NB"]
    consts = host_consts(SEQ)
    prm = host_params(inp, L)
    maps = []
    for c in range(n_cores):
        b0 = c * NB
        cv = np.concatenate([inp["c"][b0:b0 + NB], inp["c_ctx"][None, :]], axis=0).astype(np.float32)
        m = {
            "x": np.ascontiguousarray(inp["x"][b0:b0 + NB], np.float32),
            "ctx": np.ascontiguousarray(inp["ctx"][b0:b0 + NB], np.float32),
            "cT": np.ascontiguousarray(cv.reshape(NB + 1, KD, 128).transpose(2, 1, 0)),
            "w_mod": np.ascontiguousarray(inp["w_mod"][:L], np.float32),
            "w_in": np.ascontiguousarray(inp["w_in"][:L], np.float32),
            "w_out": np.ascontiguousarray(inp["w_out"][:L], np.float32),
        }
        m.update(prm)
        for k, v in consts.items():
            m["c_" + k] = v
        maps.append(m)
    return maps


_NC_CACHE = {}


def kernel(**inputs):
    inp = {k: np.asarray(v) for k, v in inputs.items()}
    B, SEQ, _ = inp["x"].shape
    CTX = inp["ctx"].shape[1]
    L = inp["w_in"].shape[0]
    n_cores = 8
    NB = B // n_cores
    cfg = dict(SEQ=SEQ, CTX=CTX, L=L, NB=NB)
    key = (SEQ, CTX, L, NB)
    if key not in _NC_CACHE:
        _NC_CACHE[key] = build(cfg)
    nc = _NC_CACHE[key]
    maps = make_in_maps(inp, cfg, n_cores)
    res = run_bass_kernel_spmd(nc, maps, core_ids=list(range(n_cores)))
    out = np.concatenate([np.asarray(r["y"]) for r in res.results], axis=0)
    return out.astype(np.float32)
```

```python
import numpy as np
import concourse.bass as bass
import concourse.mybir as mybir
from concourse.bass_utils import run_bass_kernel_spmd

F32 = mybir.dt.float32
BF16 = mybir.dt.bfloat16
AF = mybir.ActivationFunctionType
ALU = mybir.AluOpType

ENGS = ("sync", "act", "dve", "pool", "pe")


class _Rec:
    __slots__ = ("w", "r")

    def __init__(self):
        self.w = None
        self.r = {}


class Tl:
    def __init__(self, h, name, excl=False):
        self.h = h
        self.name = name
        self.excl = excl
        self.whole = _Rec()
        self.sub = {}

    def __getitem__(self, idx):
        return self.h[idx]


class Op:
    __slots__ = ("eng", "idx", "fn", "deps", "is_dma", "signal", "cnt", "slot", "target")

    def __init__(self, eng, idx, fn, is_dma):
        self.eng = eng
        self.idx = idx
        self.fn = fn
        self.deps = []
        self.is_dma = is_dma
        self.signal = False
        self.cnt = 0
        self.slot = None
        self.target = None


class Prog:
    NSLOT = 12

    def __init__(self, nc):
        self.nc = nc
        self.ops = {e: [] for e in ENGS}
        self.ndma = {e: 0 for e in ENGS}
        self.out_dmas = []

    def sb(self, name, shape, dt):
        return Tl(self.nc.alloc_sbuf_tensor(name, list(shape), dt), name)

    def ps(self, name, shape, dt=F32):
        return Tl(self.nc.alloc_psum_tensor(name, list(shape), dt), name, excl=True)

    def dram(self, name, shape, dt, kind="Internal"):
        return Tl(self.nc.dram_tensor(name, list(shape), dt, kind=kind), name)

    @staticmethod
    def _norm(a):
        if isinstance(a, Tl):
            return a, None
        return a

    def op(self, eng, fn, reads=(), writes=(), dma=False):
        lst = self.ops[eng]
        nr, nw = [], []
        for a in reads:
            t, k = self._norm(a)
            (nw if t.excl else nr).append((t, None) if t.excl else (t, k))
        for a in writes:
            t, k = self._norm(a)
            nw.append((t, None) if t.excl else (t, k))
        reads, writes = nr, nw
        o = Op(eng, len(lst), fn, dma)
        deps = set()
        for a in reads:
            t, k = self._norm(a)
            recs = [t.whole] if k is None else [t.whole, t.sub.get(k)]
            if k is None:
                recs += list(t.sub.values())
            for rc in recs:
                if rc is not None and rc.w is not None:
                    deps.add(rc.w)
        for a in writes:
            t, k = self._norm(a)
            recs = [t.whole] if k is None else [t.whole, t.sub.get(k)]
            if k is None:
                recs += list(t.sub.values())
            for rc in recs:
                if rc is None:
                    continue
                if rc.w is not None:
                    deps.add(rc.w)
                for x in rc.r.values():
                    deps.add(x)
        for d in deps:
            if d is o:
                continue
            if d.eng == eng and not d.is_dma and not dma and eng == "pe":
                continue
            o.deps.append(d)
            d.signal = True
        for a in reads:
            t, k = self._norm(a)
            rc = t.whole if k is None else t.sub.setdefault(k, _Rec())
            rc.r[eng if not dma else (eng, "dma", o.idx)] = o
            if dma:
                pass
        for a in writes:
            t, k = self._norm(a)
            if k is None:
                t.sub = {}
                t.whole = _Rec()
                t.whole.w = o
            else:
                rc = t.sub.setdefault(k, _Rec())
                rc.w = o
                rc.r = {}
        lst.append(o)
        return o

    def dma(self, eng, out_ap, in_ap, reads, writes, is_out=False, **kw):
        o = self.op(eng, lambda e: e.dma_start(out=out_ap, in_=in_ap, **kw), reads, writes, dma=True)
        if is_out:
            self.out_dmas.append(o)
        return o

    def emit(self):
        nc = self.nc
        sems = {e: nc.alloc_semaphore("sem_" + e) for e in ENGS}
        dsems = {}
        for e in ENGS:
            if any(o.is_dma for o in self.ops[e]):
                dsems[e] = [nc.alloc_semaphore("dsem_%s_%d" % (e, i)) for i in range(self.NSLOT)]
        for e in ENGS:
            c = 0
            nd = 0
            for o in self.ops[e]:
                if o.is_dma:
                    o.slot = nd % self.NSLOT
                    o.target = 16 * (nd // self.NSLOT + 1)
                    nd += 1
                else:
                    if o.signal:
                        c += 1
                    o.cnt = c
        engobj = {"sync": "sync", "act": "scalar", "dve": "vector", "pool": "gpsimd", "pe": "tensor"}
        prog = self

        def run_engine(ename, e):
            waited = {}

            def wait(sem, key, val):
                if waited.get(key, 0) >= val:
                    return
                waited[key] = val
                e.wait_ge(sem, val)

            for o in prog.ops[ename]:
                for d in o.deps:
                    if d.is_dma:
                        wait(dsems[d.eng][d.slot], (d.eng, d.slot), d.target)
                    else:
                        wait(sems[d.eng], d.eng, d.cnt)
                if o.is_dma:
                    if o.target > 16:
                        wait(dsems[ename][o.slot], (ename, o.slot), o.target - 16)
                    ins = o.fn(e)
                    ins.then_inc(dsems[ename][o.slot], 16)
                else:
                    ins = o.fn(e)
                    if o.signal:
                        ins.then_inc(sems[ename], 1)
            if ename == "sync":
                for o in prog.out_dmas:
                    wait(dsems[o.eng][o.slot], (o.eng, o.slot), o.target)

        with nc.Block() as block:
            @block.sync
            def _(e):
                run_engine("sync", e)

            @block.scalar
            def _(e):
                run_engine("act", e)

            @block.vector
            def _(e):
                run_engine("dve", e)

            @block.gpsimd
            def _(e):
                run_engine("pool", e)

            @block.tensor
            def _(e):
                run_engine("pe", e)


import math

AX = mybir.AxisListType

D = 1024
KD = 8
PIN = 5408
C_RV, C_RW, C_GQ, C_GV, C_GAD, C_GATE = 512, 1024, 2816, 3328, 3840, 3872
PF_RQ, PF_RK, PF_RW, PF_GQ, PF_GK, PF_GATE, PF_ROWS = 0, 256, 512, 2304, 2560, 2816, 4352
NTAB = 80


class Ring:
    def __init__(self, tiles):
        self.t = tiles
        self.i = 0

    def next(self):
        x = self.t[self.i % len(self.t)]
        self.i += 1
        return x


class Arena:
    def __init__(self, P, nbytes, name, junk):
        self.P = P
        self.h = P.nc.alloc_sbuf_tensor(name, [128, nbytes // 4], F32)
        self.n = nbytes
        self.off = 0
        self.live = []
        self.bar = None
        self.junk = junk

    def reset(self, extra=()):
        tl = self.live + list(extra)
        junk = self.junk
        self.bar = self.P.op("pool", lambda e: e.memset(junk[0:1, 0:4], 0.0), [], tl + [junk])
        self.live = []
        self.off = 0
        for t in extra:
            t.whole.w = self.bar

    def alloc(self, name, shape, dt):
        esz = 2 if dt == BF16 else 4
        n = 1
        for s in shape[1:]:
            n *= s
        nb = (n * esz + 31) // 32 * 32
        ap = self.h[:, self.off // 4:(self.off + nb) // 4]
        if dt == BF16:
            ap = ap.bitcast(BF16)
        ap = ap[:, 0:n]
        if len(shape) > 2:
            names = ["a%d" % i for i in range(len(shape) - 1)]
            kw = {names[i]: shape[i + 1] for i in range(len(names) - 1)}
            ap = ap.rearrange("p (%s) -> p %s" % (" ".join(names), " ".join(names)), **kw)
        self.off += nb
        assert self.off <= self.n, (name, self.off, self.n)
        t = Tl(ap, name)
        t.whole.w = self.bar
        self.live.append(t)
        return t


def host_consts(SEQ):
    c = {}
    idx = np.arange(128)
    c["ident"] = np.eye(128, dtype=np.float32)
    j = idx[:, None]
    i = idx[None, :]
    mg = np.zeros((128, 2, 256), np.float32)
    mg[:, 0, :] = np.tile((j <= i).astype(np.float32), (1, 2))
    mg[:, 1, :] = np.tile((j >= i).astype(np.float32), (1, 2))
    c["maskg"] = mg
    tg = np.zeros((128, 2, 128), np.float32)
    tg[:, 0] = (j <= i) * (-1.0 / 16)
    tg[:, 1] = (j >= i) * (-1.0 / 16)
    c["trig"] = tg
    cw = -np.exp(-0.5)
    same = (j // 64) == (i // 64)
    tr = np.zeros((128, 2, 384), np.float32)
    tr[:, 0, 0:128] = same & (j <= i)
    tr[:, 0, 128:256] = same & (j < i)
    tr[:, 0, 256:384] = same & (j > i)
    tr[:, 1, 0:128] = same & (j >= i)
    tr[:, 1, 128:256] = same & (j > i)
    tr[:, 1, 256:384] = same & (j < i)
    c["trir"] = (tr * cw).astype(np.float32)
    a = (idx % 64)[:, None]
    b = (idx % 64)[None, :]
    mk = np.zeros((128, 2, 5, 128), np.float32)
    mk[:, 0, 0] = -1.0 * (same & (a < b))
    mk[:, 0, 1] = -1.0 * (same & (a <= b))
    mk[:, 0, 2] = same & (a < b)
    mk[:, 0, 3] = same & (a <= b)
    mk[:, 0, 4] = -1.0 * (same & (b < a))
    mk[:, 1, 0] = -1.0 * (same & (a > b))
    mk[:, 1, 1] = -1.0 * (same & (a >= b))
    mk[:, 1, 2] = same & (a > b)
    mk[:, 1, 3] = same & (a >= b)
    mk[:, 1, 4] = -1.0 * (same & (b > a))
    c["mk"] = mk
    hs = np.zeros((128, 2), np.float32)
    hs[:64, 0] = 1
    hs[64:, 1] = 1
    c["hsel"] = hs
    c["blk1"] = same.astype(np.float32)
    rm = np.zeros((128, 128), np.float32)
    for base in range(0, 128, 32):
        for q in range(16):
            rm[base + 16 + q, base + q] = -1.0
            rm[base + q, base + 16 + q] = 1.0
    c["rm"] = rm
    t = np.arange(SEQ)
    rows = (t // 64).astype(np.float32)
    cols = (t % 64).astype(np.float32)
    nf = 16
    inv = (np.float32(1.0) / (np.float32(10000.0) ** (np.arange(nf, dtype=np.float32) / np.float32(nf)))).astype(np.float32)
    cos = np.zeros((128, SEQ), np.float32)
    sin = np.zeros((128, SEQ), np.float32)
    for p in range(128):
        dd = p % 64
        pos = rows if dd < 32 else cols
        ang = (pos * inv[dd % 16]).astype(np.float32)
        cos[p] = np.cos(ang)
        sin[p] = np.sin(ang)
    c["ropec"] = cos
    c["ropes"] = sin
    pos = np.zeros((128, 2, 128), np.float32)
    pos[:, 0, :] = np.arange(128) + 1
    pos[:, 1, :] = 128 - np.arange(128)
    c["pos"] = pos
    return {k: np.ascontiguousarray(v, dtype=np.float32) for k, v in c.items()}


def host_params(inp, L):
    def fm(v, n):
        return np.asarray(v, np.float32).reshape(n, 128).T
    tab = np.zeros((L, 128, NTAB), np.float32)
    rows = np.zeros((L, 4096), np.float32)
    for l in range(L):
        tab[l, :, 0:8] = fm(inp["g_pre"][l], 8)
        tab[l, :, 8:16] = fm(inp["b_mod"][l, 0:1024], 8)
        tab[l, :, 16:24] = fm(inp["b_mod"][l, 1024:2048], 8)
        tab[l, :, 24:38] = fm(inp["rwkv_mu"][l, 0], 14)
        tab[l, :, 38:52] = fm(inp["rwkv_mu"][l, 1], 14)
        tab[l, :, 52:56] = fm(inp["rwkv_k_k"][l], 4)
        tab[l, :, 56:60] = fm(inp["rwkv_k_a"][l], 4)
        tab[l, :, 60:64] = fm(inp["rwkv_r_k"][l].reshape(512), 4)
        tab[l, :, 64:68] = fm(inp["rwkv_a0"][l, 0], 4)
        tab[l, :, 68:72] = fm(inp["rwkv_a0"][l, 1], 4)
        for d in range(2):
            for hp in range(2):
                tab[l, 0:64, 72 + d * 2 + hp] = inp["ret_decay_logit"][l, d, hp * 2]
                tab[l, 64:128, 72 + d * 2 + hp] = inp["ret_decay_logit"][l, d, hp * 2 + 1]
        rows[l, 0:512] = inp["ret_gn"][l]
        rows[l, 512:1024] = inp["rwkv_gn_w"][l]
        rows[l, 1024:1536] = inp["gla_gn"][l]
        rows[l, 1536:2048] = inp["rwkv_gn_b"][l]
        rows[l, 2048:3072] = inp["b_mod"][l, 2048:3072]
        rows[l, 3072:4096] = inp["g_post"][l]
    wup = np.concatenate([inp["rwkv_w_up"][:L], inp["rwkv_w0"][:L, :, None, :]], axis=2).astype(np.float32)
    aup = np.ascontiguousarray(inp["rwkv_a_up"][:L], np.float32)
    aupg = np.concatenate([inp["gla_a_up"][:L], inp["gla_a_b"][:L, :, None, :]], axis=2).astype(np.float32)
    return dict(tab=tab, rows=rows, wup=np.ascontiguousarray(wup), aup=aup, aupg=np.ascontiguousarray(aupg))


def build(cfg):
    SEQ, CTX, L, NB = cfg["SEQ"], cfg["CTX"], cfg["L"], cfg["NB"]
    dbg = cfg.get("dbg", False)
    stop_after = cfg.get("stop_after", None)
    T = SEQ + CTX
    NT = T // 128
    CT = CTX // 128
    NV = NB + 1
    nc = bass.Bass("TRN2", target_bir_lowering=False)
    P = Prog(nc)
    EI = "ExternalInput"
    x_in = P.dram("x", [NB, SEQ, D], F32, EI)
    ctx_in = P.dram("ctx", [NB, CTX, D], F32, EI)
    cT_in = P.dram("cT", [128, KD, NV], F32, EI)
    wmod = P.dram("w_mod", [L, D, 3 * D], F32, EI)
    win = P.dram("w_in", [L, D, PIN], F32, EI)
    wout = P.dram("w_out", [L, 1536, D], F32, EI)
    tab_in = P.dram("tab", [L, 128, NTAB], F32, EI)
    rows_in = P.dram("rows", [L, 4096], F32, EI)
    wup_in = P.dram("wup", [L, 2, 65, 512], F32, EI)
    aup_in = P.dram("aup", [L, 2, 64, 512], F32, EI)
    aupg_in = P.dram("aupg", [L, 2, 17, 256], F32, EI)
    cst = {}
    for nm, shp in (("ident", [128, 128]), ("maskg", [128, 2, 256]), ("trig", [128, 2, 128]), ("trir", [128, 2, 384]),
                    ("mk", [128, 2, 5, 128]), ("hsel", [128, 2]), ("blk1", [128, 128]), ("rm", [128, 128]),
                    ("ropec", [128, SEQ]), ("ropes", [128, SEQ]), ("pos", [128, 2, 128])):
        cst[nm] = P.dram("c_" + nm, shp, F32, EI)
    okind = "ExternalOutput"
    y_out = P.dram("y", [NB, SEQ, D], F32, okind)
    skind = okind if dbg else "Internal"
    xs = P.dram("xs", [NB, T, D], F32, skind)
    PF = P.dram("PF", [NB, PF_ROWS, T], BF16, skind)
    ADG = P.dram("ADG", [NB, 32, T], F32, skind)
    VT = P.dram("VT", [NB, T, 1536], BF16, skind)
    RW = P.dram("RW", [NB, 1664, T], BF16, skind)
    RWF = P.dram("RWF", [NB, 128, T], F32, skind)
    OD = [P.dram("OD%d" % d, [NB, T, 1536], BF16, skind) for d in range(2)]
    CF = [P.dram("CF%d" % d, [NB, T, 8], F32, skind) for d in range(2)]
    GD = P.dram("GD", [L, NV, D], F32, skind)

    WIN = P.sb("WIN", [128, KD, PIN], BF16)
    WOUT = P.sb("WOUT", [128, 12, D], BF16)
    IDB = P.sb("IDB", [128, 128], BF16)
    MASKG = P.sb("MASKG", [128, 2, 256], BF16)
    TRIG = P.sb("TRIG", [128, 2, 128], F32)
    TRIR = P.sb("TRIR", [128, 2, 384], F32)
    MK = P.sb("MK", [128, 2, 5, 128], BF16)
    HSEL = P.sb("HSEL", [128, 2], BF16)
    BLK1 = P.sb("BLK1", [128, 128], BF16)
    RMB = P.sb("RMB", [128, 128], BF16)
    POS = P.sb("POS", [128, 2, 128], F32)
    TAB = P.sb("TAB", [128, NTAB], F32)
    TAB2 = P.sb("TAB2", [128, 32], F32)
    ROWS = P.sb("ROWS", [128, 2048], F32)
    GROW = P.sb("GROW", [128, D], F32)
    WUP = P.sb("WUP", [128, 2, 512], F32)
    AUP = P.sb("AUP", [128, 2, 512], BF16)
    AUPG = P.sb("AUPG", [128, 2, 256], F32)
    REB = P.sb("REB", [128, 2, 2, 128], F32)
    RENB = P.sb("RENB", [128, 2, 2, 128], F32)
    ABF = P.sb("ABF", [128, 2, KD, NV], F32)
    SC = P.sb("SC", [128, KD, NV], F32)
    ONES = P.sb("ONES", [128, 1], F32)
    EPS = P.sb("EPS", [128, 4], F32)
    JUNK = P.sb("JUNK", [128, 8], F32)
    BK = [P.ps("BK%d" % i, [128, 512], F32) for i in range(8)]

    def bkf(i):
        return BK[i][:, :]

    def bkb(i):
        return BK[i][:, :].bitcast(BF16)

    _rem = int(nc.sbuf_bytes_remaining)
    _asz = ((_rem - 2048) // 1024) * 1024
    if cfg.get("dbg"):
        print("sbuf remaining", _rem, "arena", _asz)
    AR = Arena(P, _asz, "ARENA", JUNK)
    AR2 = Arena(P, 4, "ARENA2dummy", JUNK)
    AR2.h = WIN.h[:, :, :].rearrange("p k c -> p (k c)").bitcast(F32)
    AR2.n = KD * PIN * 2

    def TT(eng, out, in0, in1, op, R, W):
        P.op(eng, lambda e: e.tensor_tensor(out=out, in0=in0, in1=in1, op=op), R, W)

    def TS(eng, out, in0, s1, s2, op0, op1, R, W):
        if s2 is None:
            P.op(eng, lambda e: e.tensor_scalar(out=out, in0=in0, scalar1=s1, scalar2=None, op0=op0), R, W)
        else:
            P.op(eng, lambda e: e.tensor_scalar(out=out, in0=in0, scalar1=s1, scalar2=s2, op0=op0, op1=op1), R, W)

    def STT(out, in0, sc, in1, op0, op1, R, W):
        P.op("dve", lambda e: e.scalar_tensor_tensor(out=out, in0=in0, scalar=sc, in1=in1, op0=op0, op1=op1), R, W)

    def ACT(out, in_, func, R, W, bias=None, scale=1.0, accum=None):
        kw = {}
        if bias is not None:
            kw["bias"] = bias
        if accum is not None:
            kw["accum_out"] = accum
        P.op("act", lambda e: e.activation(out=out, in_=in_, func=func, scale=scale, **kw), R, W)

    def MM(out, lhsT, rhs, R, W, start=True, stop=True):
        P.op("pe", lambda e: e.matmul(out, lhsT=lhsT, rhs=rhs, start=start, stop=stop), R, W)

    def TRP(out, in_, R, W):
        idb = IDB[:, :]
        P.op("pe", lambda e: e.transpose(out, in_, idb), list(R) + [IDB], W)

    def CP(eng, out, in_, R, W):
        if eng == "act":
            P.op("act", lambda e: e.copy(out=out, in_=in_), R, W)
        else:
            P.op(eng, lambda e: e.tensor_copy(out=out, in_=in_), R, W)

    def MS(eng, ap, val, W):
        P.op(eng, lambda e: e.memset(ap, val), [], W)

    def LD(out, in_, R, W, eng="sync", **kw):
        P.dma(eng, out, in_, R, W, **kw)

    def STO(out, in_, R, W, is_out=False, eng="pool"):
        P.dma(eng, out, in_, R, W, is_out=is_out)

    for (tl, nm) in ((IDB, "ident"), (MASKG, "maskg"), (MK, "mk"), (HSEL, "hsel"), (BLK1, "blk1"), (RMB, "rm")):
        full = tuple(slice(None) for _ in tl.h.shape)
        P.dma("pool", tl[full], cst[nm][full], [cst[nm]], [tl])
    for (tl, nm) in ((TRIG, "trig"), (TRIR, "trir"), (POS, "pos")):
        full = tuple(slice(None) for _ in tl.h.shape)
        LD(tl[full], cst[nm][full], [cst[nm]], [tl])
    MS("dve", ONES[:, :], 1.0, [ONES])
    MS("dve", EPS[:, 0:1], 1e-6, [(EPS, 0)])
    MS("dve", EPS[:, 1:2], 64e-5, [(EPS, 1)])
    MS("dve", EPS[:, 2:3], 0.0, [(EPS, 2)])
    MS("dve", JUNK[:, :], 0.0, [JUNK])
    MS("pool", AUP[:, :, :], 0.0, [AUP])
    MS("pool", AUPG[:, :, :], 0.0, [AUPG])
    LD(SC[:, :, :], cT_in[:, :, :], [cT_in], [SC])
    ACT(SC[:, :, :], SC[:, :, :], AF.Silu, [SC], [SC])

    groups = []
    for a in range(0, CT, 4):
        groups.append((list(range(a, min(a + 4, CT))), True))
    for a in range(CT, NT, 4):
        groups.append((list(range(a, min(a + 4, NT))), False))

    def order(d):
        if d == 0:
            return list(range(NT))
        return list(range(CT - 1, -1, -1)) + list(range(NT - 1, CT - 1, -1))

    def kPF(s, tt, ch):
        return (PF, (s, tt, ch))

    skip = cfg.get("skip", ())

    def layer_setup(l):
        AR.reset(BK)
        AR2.reset([WIN])
        LD(TAB[:, :], tab_in[l, :, :], [tab_in], [TAB])
        LD(ROWS[:, :].rearrange("p (o n) -> p o n", o=1), rows_in[l:l + 1, 0:2048].partition_broadcast(128), [rows_in], [ROWS])
        LD(WUP[0:65, :, :], wup_in[l].rearrange("d r c -> r d c"), [wup_in], [WUP])
        P.dma("pool", AUP[0:64, 0, :], aup_in[l, 0], [aup_in], [AUP])
        P.dma("pool", AUP[64:128, 1, :], aup_in[l, 1], [aup_in], [AUP])
        LD(AUPG[0:16, :, :], aupg_in[l, :, 0:16, :].rearrange("d r c -> r d c"), [aupg_in], [AUPG])
        LD(AUPG[32:33, :, :], aupg_in[l, :, 16:17, :].rearrange("d r c -> r d c"), [aupg_in], [AUPG])
        if "setup" in skip:
            return
        wv = win[l].rearrange("(k p) c -> p k c", p=128)
        for c0 in range(0, PIN, 1024):
            c1 = min(PIN, c0 + 1024)
            P.dma("pool", WIN[:, :, c0:c1], wv[:, :, c0:c1], [win], [WIN])
        wo = wout[l].rearrange("(k p) c -> p k c", p=128)
        P.dma("pool", WOUT[:, :, :], wo[:, :, :], [wout], [WOUT])
        TS("dve", TAB2[:, 0:14], TAB[:, 24:38], -1.0, 1.0, ALU.mult, ALU.add, [TAB], [(TAB2, "c0")])
        TT("dve", TAB2[:, 0:14], TAB2[:, 0:14], TAB[:, 38:52], ALU.subtract, [TAB, (TAB2, "c0")], [(TAB2, "c0")])
        TS("dve", TAB2[:, 14:18], TAB[:, 56:60], -1.0, 1.0, ALU.mult, ALU.add, [TAB], [(TAB2, "omka")])
        ACT(TAB2[:, 26:30], TAB[:, 72:76], AF.Exp, [TAB], [(TAB2, "t")], scale=-1.0)
        ACT(TAB2[:, 22:26], TAB2[:, 26:30], AF.Ln, [(TAB2, "t"), ONES], [(TAB2, "nlg")], bias=ONES[:, 0:1])
        TS("dve", TAB2[:, 18:22], TAB2[:, 22:26], -1.0, None, ALU.mult, None, [(TAB2, "nlg")], [(TAB2, "lg")])
        for d in range(2):
            for hp in range(2):
                ci = d * 2 + hp
                ACT(REB[:, d, hp, :], POS[:, d, :], AF.Exp, [POS, (TAB2, "lg")], [(REB, ci)], scale=TAB2[:, 18 + ci:19 + ci])
                ACT(RENB[:, d, hp, :], POS[:, d, :], AF.Exp, [POS, (TAB2, "nlg")], [(RENB, ci)], scale=TAB2[:, 22 + ci:23 + ci])
        if "mod" in skip:
            return
        WM = Ring([AR.alloc("WM%d" % i, [128, KD, 256], F32) for i in range(2)])
        R1 = AR.alloc("R1", [128, 2048], F32)
        GR = AR.alloc("GR", [128, NV, 1024], F32)
        SCB = AR.alloc("SCB", [128, KD, NV, 128], F32)
        CP("dve", SCB[:, :, :, :], SC[:, :, :].rearrange("p k (v o) -> p k v o", o=1).broadcast_to([128, KD, NV, 128]), [SC], [SCB])
        LD(R1[0:1, :], rows_in[l:l + 1, 2048:4096], [rows_in], [R1])
        wmv = wmod[l].rearrange("(k p) c -> p k c", p=128)
        psm = BK[0]
        psmv = bkf(0)[:, 0:16 * NV].rearrange("p (c v) -> p c v", v=NV)
        for pc in range(12):
            wm = WM.next()
            LD(wm[:, :, :], wmv[:, :, pc * 256:(pc + 1) * 256], [wmod], [wm])
            if pc < 8:
                for cc in range(2):
                    ch = pc * 2 + cc
                    for k in range(KD):
                        MM(psmv[:, ch, :], wm[:, k, cc * 128:(cc + 1) * 128], SC[:, k, :], [wm, SC], [(psm, "m")], start=(k == 0), stop=(k == KD - 1))
            else:
                cg = pc - 8
                for v in range(NV if "gate" not in skip else 0):
                    pg = bkf(1 + (v % 2))
                    for k in range(KD):
                        MM(pg[:, 0:256], SCB[:, k, v, :], wm[:, k, :], [wm, SCB], [BK[1 + (v % 2)]], start=(k == 0), stop=(k == KD - 1))
                    cs_ = slice(cg * 256, (cg + 1) * 256)
                    cs2 = slice(1024 + cg * 256, 1024 + (cg + 1) * 256)
                    TT("dve", GR[0:1, v, cs_], pg[0:1, 0:256], R1[0:1, cs_], ALU.add, [BK[1 + (v % 2)], R1], [(GR, (v, cg))])
                    TT("dve", GR[0:1, v, cs_], GR[0:1, v, cs_], R1[0:1, cs2], ALU.mult, [(GR, (v, cg)), R1], [(GR, (v, cg))])
        TT("dve", ABF[:, 1, :, :], psmv[:, 0:8, :], TAB[:, 8:16].rearrange("p (k o) -> p k o", o=1).broadcast_to([128, KD, NV]), ALU.add, [(psm, "m"), TAB], [(ABF, "B")])
        TT("dve", ABF[:, 0, :, :], psmv[:, 8:16, :], TAB[:, 16:24].rearrange("p (k o) -> p k o", o=1).broadcast_to([128, KD, NV]), ALU.add, [(psm, "m"), TAB], [(ABF, "A")])
        STT(ABF[:, 0, :, :], ABF[:, 0, :, :], 1.0, TAB[:, 0:8].rearrange("p (k o) -> p k o", o=1).broadcast_to([128, KD, NV]), ALU.add, ALU.mult, [(ABF, "A"), TAB], [(ABF, "A")])
        for v in range(NV):
            STO(GD[l, v:v + 1, :], GR[0:1, v, :], [GR], [(GD, (l, v))])

    def stageA(l, s):
        AR.reset(BK)
        xin = Ring([AR.alloc("xin%d" % i, [128, D], F32) for i in range(2)])
        xn = Ring([AR.alloc("xn%d" % i, [128, D], BF16) for i in range(2)])
        hT = Ring([AR.alloc("hT%d" % i, [128, KD, 512], BF16) for i in range(2)])
        stg = Ring([AR.alloc("stg%d" % i, [128, 512], BF16) for i in range(4)])
        stf = AR.alloc("stf", [128, 512], F32)
        cosT = AR.alloc("cosT", [128, 512], F32)
        sinT = AR.alloc("sinT", [128, 512], F32)
        rt1 = AR.alloc("rt1", [128, 512], F32)
        rt2 = AR.alloc("rt2", [128, 512], F32)
        sst = Ring([AR.alloc("sst%d" % i, [128, 4], F32) for i in range(2)])
        junkA = AR.alloc("junkA", [128, D], BF16)
        psA = Ring([2, 3, 4, 5])
        evi = [0]
        hbuf = {}

        def prepH(gi):
            tiles, isctx = groups[gi]
            N = 128 * len(tiles)
            t0 = tiles[0] * 128
            h = hT.next()
            hbuf[gi] = h
            vec = NB if isctx else s
            for j, tt in enumerate(tiles):
                xi = xin.next()
                xb = xn.next()
                st = sst.next()
                if l == 0:
                    if isctx:
                        src, srct = ctx_in[s, tt * 128:(tt + 1) * 128, :], ctx_in
                    else:
                        src, srct = x_in[s, (tt - CT) * 128:(tt - CT + 1) * 128, :], x_in
                    LD(xi[:, :], src, [srct], [xi])
                else:
                    LD(xi[:, :], xs[s, tt * 128:(tt + 1) * 128, :], [(xs, (s, tt))], [xi])
                if "norm1" not in skip:
                    ACT(junkA[:, :], xi[:, :], AF.Square, [xi], [junkA, (st, 0)], accum=st[:, 0:1])
                else:
                    MS("dve", st[:, 0:1], 1024.0, [(st, 0)])
                if "norm2" not in skip:
                    ACT(st[:, 1:2], st[:, 0:1], AF.Sqrt, [(st, 0), (EPS, 0)], [(st, 1)], bias=EPS[:, 0:1], scale=1.0 / D)
                else:
                    MS("dve", st[:, 1:2], 1.0, [(st, 1)])
                P.op("dve", lambda e, o=st[:, 2:3], i_=st[:, 1:2]: e.reciprocal(out=o, in_=i_), [(st, 1)], [(st, 2)])
                TS("dve", xb[:, :], xi[:, :], st[:, 2:3], None, ALU.mult, None, [xi, (st, 2)], [xb])
                pb_ = (tt % 2)
                ptv = bkb(pb_).rearrange("p (k t) -> p k t", k=KD)
                for k in range(KD if "tr" not in skip else 0):
                    TRP(ptv[:, k, :], xb[:, k * 128:(k + 1) * 128], [xb], [(BK[pb_], k)])
                for k in range(KD if "ev" not in skip else 0):
                    evm = cfg.get("evmode", "mix")
                    if (k % 2 == 0 and evm == "mix") or evm == "act":
                        ACT(h[:, k, j * 128:(j + 1) * 128], ptv[:, k, :], AF.Identity, [(BK[pb_], k), (ABF, "A"), (ABF, "B")], [(h, (k, j))],
                            bias=ABF[:, 1, k, vec:vec + 1], scale=ABF[:, 0, k, vec:vec + 1])
                    else:
                        TS("dve", h[:, k, j * 128:(j + 1) * 128], ptv[:, k, :], ABF[:, 0, k, vec:vec + 1], ABF[:, 1, k, vec:vec + 1], ALU.mult, ALU.add,
                           [(BK[pb_], k), (ABF, "A"), (ABF, "B")], [(h, (k, j))])

        def proj(gi):
            tiles, isctx = groups[gi]
            N = 128 * len(tiles)
            t0 = tiles[0] * 128
            h = hbuf[gi]
            if not isctx:
                LD(cosT[:, 0:N], cst["ropec"][:, t0 - CTX:t0 - CTX + N], [cst["ropec"]], [cosT])
                LD(sinT[:, 0:N], cst["ropes"][:, t0 - CTX:t0 - CTX + N], [cst["ropes"]], [sinT])
            chunks = []
            for i in range(4):
                chunks.append((i * 128, 128, "rope", PF_RQ + i * 128))
            for i in range(14):
                chunks.append((C_RW + i * 128, 128, "fm", PF_RW + i * 128))
            for i in range(4):
                chunks.append((C_GQ + i * 128, 128, "fm", PF_GQ + i * 128))
            if "ad" not in skip:
                chunks.append((C_GAD - 96, 128, "ad", 0))
            for i in range(12):
                chunks.append((C_GATE + i * 128, 128, "fm", PF_GATE + i * 128))
            for (c0, cw, kind, dst) in (chunks if "fm" not in skip else []):
                b = psA.next()
                ps = bkf(b)
                for k in range(KD):
                    MM(ps[0:cw, 0:N], WIN[:, k, c0:c0 + cw], h[:, k, 0:N], [WIN, h], [BK[b]], start=(k == 0), stop=(k == KD - 1))
                keys = [kPF(s, tt, dst // 128) for tt in tiles]
                if kind == "ad":
                    CP("dve", stf[96:128, 0:N], ps[96:128, 0:N], [BK[b]], [stf])
                    STO(ADG[s, :, t0:t0 + N], stf[96:128, 0:N], [stf], [(ADG, (s, tt)) for tt in tiles])
                    continue
                sg = stg.next()
                if kind == "rope" and not isctx and "rope" not in skip:
                    sb2 = stg.next()
                    CP("act", sb2[:, 0:N], ps[:, 0:N], [BK[b]], [sb2])
                    ps2 = bkf(6)
                    MM(ps2[:, 0:N], RMB[:, :], sb2[:, 0:N], [RMB, sb2], [BK[6]])
                    TT("dve", rt1[:, 0:N], ps[:, 0:N], cosT[:, 0:N], ALU.mult, [BK[b], cosT], [rt1])
                    TT("dve", rt2[:, 0:N], ps2[:, 0:N], sinT[:, 0:N], ALU.mult, [BK[6], sinT], [rt2])
                    TT("dve", sg[:, 0:N], rt1[:, 0:N], rt2[:, 0:N], ALU.add, [rt1, rt2], [sg])
                else:
                    evi[0] += 1
                    CP("act" if evi[0] % 2 else "dve", sg[:, 0:N], ps[:, 0:N], [BK[b]], [sg])
                STO(PF[s, dst:dst + cw, t0:t0 + N], sg[0:cw, 0:N], [sg], keys)
            for j, tt in enumerate(tiles if "tm" not in skip else []):
                for vi, vc in enumerate((C_RV, C_GV)):
                    b = psA.next()
                    ps = bkf(b)
                    for k in range(KD):
                        MM(ps[:, :], h[:, k, j * 128:(j + 1) * 128], WIN[:, k, vc:vc + 512], [WIN, h], [BK[b]], start=(k == 0), stop=(k == KD - 1))
                    sg = stg.next()
                    evi[0] += 1
                    CP("act" if evi[0] % 2 else "dve", sg[:, :], ps[:, :], [BK[b]], [sg])
                    STO(VT[s, tt * 128:(tt + 1) * 128, vi * 512:(vi + 1) * 512], sg[:, :], [sg], [(VT, (s, tt, vi))])


        prepH(0)
        for gi in range(len(groups)):
            if gi + 1 < len(groups):
                prepH(gi + 1)
            proj(gi)
    def stageG(l, s, d, ar, B):
        ar.reset([BK[b_] for b_ in B])
        SG = ar.alloc("SG", [128, 4, 128], F32)
        SGb = ar.alloc("SGb", [128, 4, 128], BF16)
        ADl = ar.alloc("ADl", [128, 128], F32)
        e1 = ar.alloc("e1", [128, 256], F32)
        sp = ar.alloc("sp", [128, 256], F32)
        NBUF = 2
        Kt = Ring([ar.alloc("Kt%d" % i, [128, 128], BF16) for i in range(NBUF)])
        QP = Ring([ar.alloc("QP%d" % i, [128, 2, 128], BF16) for i in range(NBUF)])
        Vt = Ring([ar.alloc("Vt%d" % i, [128, 256], BF16) for i in range(NBUF)])
        EBt = Ring([ar.alloc("EBt%d" % i, [128, 128], F32) for i in range(NBUF)])
        ENBt = Ring([ar.alloc("ENBt%d" % i, [128, 128], F32) for i in range(NBUF)])
        QT = Ring([ar.alloc("QT%d" % i, [128, 2, 128], BF16) for i in range(NBUF)])
        KT = Ring([ar.alloc("KT%d" % i, [128, 128], BF16) for i in range(NBUF)])
        KH = Ring([ar.alloc("KH%d" % i, [128, 128], BF16) for i in range(NBUF)])
        KHtm = Ring([ar.alloc("KHtm%d" % i, [128, 128], BF16) for i in range(NBUF)])
        STt = Ring([ar.alloc("STt%d" % i, [128, 256], BF16) for i in range(NBUF)])
        OUTt = Ring([ar.alloc("OUTt%d" % i, [128, 256], BF16) for i in range(NBUF)])
        MS("pool", SG[:, :, :], 0.0, [SG])
        MS("pool", SGb[:, :, :], 0.0, [SGb])
        MS("dve", ADl[:, :], 0.0, [ADl])
        MS("dve", ADl[32:33, :], 1.0, [ADl])
        for qp in QP.t:
            MS("pool", qp[:, :, :], 0.0, [qp])
        last = 127 if d == 0 else 0
        for tt in order(d):
            t0 = tt * 128
            LD(ADl[0:16, :], ADG[s, d * 16:(d + 1) * 16, t0:t0 + 128], [(ADG, (s, tt))], [ADl])
            MM(bkf(B[0])[:, 0:256], ADl[:, :], AUPG[:, d, :], [ADl, AUPG], [BK[B[0]]])
            ACT(e1[:, :], bkf(B[0])[:, 0:256], AF.Exp, [BK[B[0]]], [e1], scale=-1.0)
            ACT(sp[:, :], e1[:, :], AF.Ln, [e1, ONES], [sp], bias=ONES[:, 0:1])
            for hp in range(4):
                kt, qp, vt = Kt.next(), QP.next(), Vt.next()
                if hp < 2:
                    qr, kr, vc, oc = PF_RQ + hp * 128, PF_RK + hp * 128, hp * 256, hp * 256
                else:
                    qr, kr, vc, oc = PF_GQ + (hp - 2) * 128, PF_GK + (hp - 2) * 128, 512 + (hp - 2) * 256, 1024 + (hp - 2) * 256
                LD(kt[:, :], PF[s, kr:kr + 128, t0:t0 + 128], [kPF(s, tt, kr // 128)], [kt])
                LD(qp[0:64, 0, :], PF[s, qr:qr + 64, t0:t0 + 128], [kPF(s, tt, qr // 128)], [qp])
                LD(qp[64:128, 1, :], PF[s, qr + 64:qr + 128, t0:t0 + 128], [kPF(s, tt, qr // 128)], [qp])
                LD(vt[:, :], VT[s, t0:t0 + 128, vc:vc + 256], [(VT, (s, tt, vc // 512))], [vt])
                if hp < 2:
                    ebt, enbt = REB, RENB
                    eb = REB[:, d, hp, :]
                    enb = RENB[:, d, hp, :]
                    ebR = [(REB, d * 2 + hp)]
                    enbR = [(RENB, d * 2 + hp)]
                else:
                    ebt, enbt = EBt.next(), ENBt.next()
                    MM(bkf(B[0])[:, 256:384], sp[:, (hp - 2) * 128:(hp - 1) * 128], TRIG[:, d, :], [sp, TRIG], [BK[B[0]]])
                    ACT(ebt[:, :], bkf(B[0])[:, 256:384], AF.Exp, [BK[B[0]]], [ebt])
                    ACT(enbt[:, :], bkf(B[0])[:, 256:384], AF.Exp, [BK[B[0]]], [enbt], scale=-1.0)
                    eb, enb = ebt[:, :], enbt[:, :]
                    ebR, enbR = [ebt], [enbt]
                qT, kT, kH, kHtm, sT, oT = QT.next(), KT.next(), KH.next(), KHtm.next(), STt.next(), OUTt.next()
                ebb = eb.rearrange("p (o t) -> p o t", o=1).broadcast_to([128, 2, 128])
                STT(qT[:, :, :], qp[:, :, :], 0.125, ebb, ALU.mult, ALU.mult, [qp] + ebR, [qT])
                TT("pool", kT[:, :], kt[:, :], enb, ALU.mult, [kt] + enbR, [kT])
                TS("dve", kH[:, :], kT[:, :], eb[:, last:last + 1], None, ALU.mult, None, [kT] + ebR, [kH])
                TRP(bkb(B[1])[:, 0:128], kH[:, :], [kH], [BK[B[1]]])
                CP("act", kHtm[:, :], bkb(B[1])[:, 0:128], [BK[B[1]]], [kHtm])
                yield
                for h in range(2):
                    MM(bkf(B[2])[:, h * 128:(h + 1) * 128], kT[:, :], qT[:, h, :], [kT, qT], [(BK[B[2]], h)])
                TT("dve", sT[:, :], bkf(B[2])[:, 0:256], MASKG[:, d, :], ALU.mult, [BK[B[2]], MASKG], [sT])
                yield
                for h in range(2):
                    MM(bkf(B[3])[:, h * 128:(h + 1) * 128], sT[:, h * 128:(h + 1) * 128], vt[:, h * 128:(h + 1) * 128], [sT, vt], [(BK[B[3]], h)], start=True, stop=False)
                    MM(bkf(B[3])[:, h * 128:(h + 1) * 128], qT[:, h, :], SGb[:, hp, :], [qT, (SGb, hp)], [(BK[B[3]], h)], start=False, stop=True)
                CP("act", oT[:, :], bkf(B[3])[:, 0:256], [BK[B[3]]], [oT])
                STO(OD[d][s, t0:t0 + 128, oc:oc + 256], oT[:, :], [oT], [(OD[d], (s, tt, oc // 256))], eng="act")
                MM(bkf(B[1])[:, 128:384], kHtm[:, :], vt[:, :], [kHtm, vt], [BK[B[1]]])
                for h in range(2):
                    r0, r1 = h * 64, (h + 1) * 64
                    STT(SG[r0:r1, hp, :], SG[r0:r1, hp, :], eb[r0:r1, last:last + 1], bkf(B[1])[r0:r1, 128 + h * 128:128 + (h + 1) * 128], ALU.mult, ALU.add,
                        [(SG, hp), BK[B[1]]] + ebR, [(SG, hp)])
                CP("act", SGb[:, hp, :], SG[:, hp, :], [(SG, hp)], [(SGb, hp)])
                yield

    def stageRP(l, s, ar, B):
        ar.reset([BK[b_] for b_ in B])
        raw = Ring([ar.alloc("raw%d" % i, [128, 516], BF16) for i in range(3)])
        t1 = Ring([ar.alloc("t1_%d" % i, [128, 512], F32) for i in range(2)])
        t2 = Ring([ar.alloc("t2_%d" % i, [128, 512], F32) for i in range(2)])
        shb = Ring([ar.alloc("shb%d" % i, [128, 512], BF16) for i in range(3)])
        shf = Ring([ar.alloc("shf%d" % i, [128, 512], F32) for i in range(2)])
        kx = ar.alloc("kx", [128, 512], F32)
        sq = ar.alloc("sq", [128, 512], BF16)
        rn = ar.alloc("rn", [128, 512], F32)
        vsh = ar.alloc("vsh", [128, 4, 512], BF16)
        vts = Ring([ar.alloc("vts%d" % i, [128, 512], BF16) for i in range(2)])
        for gi, (tiles, isctx) in enumerate(groups):
            N = 128 * len(tiles)
            t0 = tiles[0] * 128
            lo, hi = (0, CTX) if isctx else (CTX, T)
            a0 = max(t0 - 1, lo)
            a1 = min(t0 + N + 1, hi)
            nb_t = [tt for tt in (tiles[0] - 1, tiles[-1] + 1) if lo <= tt * 128 < hi]
            for ci in range(14):
                rw = raw.next()
                if a0 == t0:
                    MS("dve", rw[:, 0:2], 0.0, [rw])
                if a1 == t0 + N:
                    MS("dve", rw[:, N:N + 2], 0.0, [rw])
                off = 1 - (t0 - a0)
                LD(rw[:, off:off + (a1 - a0)], PF[s, PF_RW + ci * 128:PF_RW + (ci + 1) * 128, a0:a1],
                   [kPF(s, tt, (PF_RW // 128) + ci) for tt in tiles + nb_t], [rw])
                u1, u2 = t1.next(), t2.next()
                ACT(u1[:, 0:N], rw[:, 1:N + 1], AF.Identity, [rw, (TAB2, "c0")], [u1], scale=TAB2[:, ci:ci + 1])
                STT(u2[:, 0:N], rw[:, 0:N], TAB[:, 24 + ci:25 + ci], u1[:, 0:N], ALU.mult, ALU.add, [rw, TAB, u1], [u2])
                if ci < 4 or ci == 13:
                    ob = shb.next()
                    STT(ob[:, 0:N], rw[:, 2:N + 2], TAB[:, 38 + ci:39 + ci], u2[:, 0:N], ALU.mult, ALU.add, [rw, TAB, u2], [ob])
                    row = ci * 128 if ci < 4 else 1536
                    STO(RW[s, row:row + 128, t0:t0 + N], ob[:, 0:N], [ob], [(RW, (s, tt, row // 128)) for tt in tiles])
                elif ci < 8:
                    hp = ci - 4
                    of = shf.next()
                    STT(of[:, 0:N], rw[:, 2:N + 2], TAB[:, 38 + ci:39 + ci], u2[:, 0:N], ALU.mult, ALU.add, [rw, TAB, u2], [of])
                    ob = shb.next()
                    CP("act", ob[:, 0:N], of[:, 0:N], [of], [ob])
                    STO(RW[s, 512 + hp * 128:512 + (hp + 1) * 128, t0:t0 + N], ob[:, 0:N], [ob], [(RW, (s, tt, 4 + hp)) for tt in tiles])
                    ACT(kx[:, 0:N], of[:, 0:N], AF.Identity, [of, TAB], [kx], scale=TAB[:, 52 + hp:53 + hp])
                    ACT(sq[:, 0:N], kx[:, 0:N], AF.Square, [kx], [sq])
                    MM(bkf(B[0])[:, 0:N], BLK1[:, :], sq[:, 0:N], [BLK1, sq], [BK[B[0]]])
                    ACT(rn[:, 0:N], bkf(B[0])[:, 0:N], AF.Ln, [BK[B[0]]], [rn])
                    ACT(rn[:, 0:N], rn[:, 0:N], AF.Exp, [rn], [rn], scale=-0.5)
                    ob2 = shb.next()
                    TT("dve", ob2[:, 0:N], kx[:, 0:N], rn[:, 0:N], ALU.mult, [kx, rn], [ob2])
                    STO(RW[s, 1024 + hp * 128:1024 + (hp + 1) * 128, t0:t0 + N], ob2[:, 0:N], [ob2], [(RW, (s, tt, 8 + hp)) for tt in tiles])
                elif ci < 12:
                    hp = ci - 8
                    STT(vsh[:, hp, 0:N], rw[:, 2:N + 2], TAB[:, 38 + ci:39 + ci], u2[:, 0:N], ALU.mult, ALU.add, [rw, TAB, u2], [(vsh, hp)])
                else:
                    of = shf.next()
                    STT(of[:, 0:N], rw[:, 2:N + 2], TAB[:, 38 + ci:39 + ci], u2[:, 0:N], ALU.mult, ALU.add, [rw, TAB, u2], [of])
                    ACT(of[:, 0:N], of[:, 0:N], AF.Tanh, [of], [of])
                    STO(RWF[s, :, t0:t0 + N], of[:, 0:N], [of], [(RWF, (s, tt)) for tt in tiles])
                yield
            for j, tt in enumerate(tiles):
                pb = B[1 + (j % 2)]
                for hp in range(4):
                    TRP(bkb(pb)[:, hp * 128:(hp + 1) * 128], vsh[:, hp, j * 128:(j + 1) * 128], [(vsh, hp)], [(BK[pb], hp)])
                vo = vts.next()
                CP("act", vo[:, :], bkb(pb)[:, 0:512], [BK[pb]], [vo])
                STO(VT[s, tt * 128:(tt + 1) * 128, 1024:1536], vo[:, :], [vo], [(VT, (s, tt, 2))])

    def stageR(l, s, d, ar, B):
        ar.reset([BK[b_] for b_ in B])
        STs = ar.alloc("STs", [128, 4, 64], F32)
        STb = ar.alloc("STb", [128, 4, 64], BF16)
        TWl = ar.alloc("TWl", [128, 128], F32)
        ADr = Ring([ar.alloc("ADr%d" % i, [128, 128], BF16) for i in range(2)])
        sig = ar.alloc("sig", [128, 512], F32)
        RK = Ring([ar.alloc("RK%d" % i, [128, 3, 128], BF16) for i in range(2)])
        E4 = Ring([ar.alloc("E4_%d" % i, [128, 4, 384], F32) for i in range(1)])
        PCT = Ring([ar.alloc("PCT%d" % i, [128, 2, 4, 2], F32) for i in range(3)])
        En = Ring([ar.alloc("En%d" % i, [128, 128], F32) for i in range(2)])
        av = Ring([ar.alloc("av%d" % i, [128, 128], F32) for i in range(4)])
        tmpv = Ring([ar.alloc("tmpv%d" % i, [128, 128], F32) for i in range(2)])
        kdv = Ring([ar.alloc("kdv%d" % i, [128, 128], F32) for i in range(2)])
        kav = Ring([ar.alloc("kav%d" % i, [128, 128], F32) for i in range(2)])
        prv = Ring([ar.alloc("prv%d" % i, [128, 128], BF16) for i in range(2)])
        cft = Ring([ar.alloc("cft%d" % i, [128, 8], F32) for i in range(2)])
        NPB = 2
        APAD = Ring([ar.alloc("APAD%d" % i, [128, 4, 2, 2, 2, 64], BF16) for i in range(NPB)])
        BPAD = Ring([ar.alloc("BPAD%d" % i, [128, 4, 2, 2, 64], BF16) for i in range(NPB)])
        KPAD = Ring([ar.alloc("KPAD%d" % i, [128, 4, 2, 2, 64], BF16) for i in range(NPB)])
        KHP = Ring([ar.alloc("KHP%d" % i, [128, 4, 2, 2, 64], BF16) for i in range(NPB)])
        BHP = Ring([ar.alloc("BHP%d" % i, [128, 4, 2, 2, 64], BF16) for i in range(NPB)])
        VST = Ring([ar.alloc("VST%d" % i, [128, 4, 64], BF16) for i in range(3)])
        Xm = [ar.alloc("Xm%d" % i, [128, 4, 128], BF16) for i in range(2)]
        Ym = [ar.alloc("Ym%d" % i, [128, 4, 128], BF16) for i in range(2)]
        Hm = [ar.alloc("Hm%d" % i, [128, 4, 128], BF16) for i in range(2)]
        HF = [ar.alloc("HF%d" % i, [128, 4, 128], BF16) for i in range(2)]
        MAKs = [ar.alloc("MAK%d" % i, [128, 4, 128], BF16) for i in range(2)]
        ARKs = [ar.alloc("ARK%d" % i, [128, 4, 128], BF16) for i in range(2)]
        ARBs = [ar.alloc("ARB%d" % i, [128, 4, 128], BF16) for i in range(2)]
        Xs = ar.alloc("Xs", [128, 4, 64], BF16)
        Us = ar.alloc("Us", [128, 4, 64], BF16)
        Oo = Ring([ar.alloc("Oo%d" % i, [128, 4, 64], BF16) for i in range(2)])
        KBt = ar.alloc("KBt", [128, 4, 2, 128], BF16)
        MS("pool", STs[:, :, :], 0.0, [STs])
        MS("pool", STb[:, :, :], 0.0, [STb])
        MS("dve", TWl[:, :], 1.0, [TWl])
        for rg in (APAD, BPAD, KPAD, KHP, BHP):
            for tl in rg.t:
                full = tuple(slice(None) for _ in tl.h.shape)
                MS("pool", tl[full], 0.0, [tl])
        lastc = 63 if d == 0 else 0
        idbb = IDB[:, :].rearrange("p (o t) -> p o t", o=1).broadcast_to([128, 4, 128])
        bP, bX, bY, bH, bS = B[2], B[0], B[1], B[2], B[3]

        def mkb(i):
            return MK[:, d, i, :].rearrange("p (o t) -> p o t", o=1).broadcast_to([128, 2, 128])

        def prep(tt, out):
            t0 = tt * 128
            LD(TWl[0:64, :], RWF[s, d * 64:(d + 1) * 64, t0:t0 + 128], [(RWF, (s, tt))], [TWl])
            MM(bkf(bP)[:, :], TWl[0:65, :], WUP[0:65, d, :], [TWl, WUP], [BK[bP]])
            ACT(sig[:, :], bkf(bP)[:, :], AF.Sigmoid, [BK[bP]], [sig])
            adl = ADr.next()
            LD(adl[:, :], RW[s, 1536:1664, t0:t0 + 128], [(RW, (s, tt, 12))], [adl])
            ap_, bp_, kp_, khp_, bhp_ = APAD.next(), BPAD.next(), KPAD.next(), KHP.next(), BHP.next()
            e4 = E4.next()
            pct = PCT.next()
            cf = cft.next()
            out.update(ap=ap_, bp=bp_, kp=kp_, khp=khp_, bhp=bhp_, pct=pct)
            a_list = []
            for hp in range(4):
                MM(bkf(bP)[:, 384:512], AUP[:, d, hp * 128:(hp + 1) * 128], adl[:, :], [AUP, adl], [BK[bP]])
                a_ = av.next()
                ACT(a_[:, :], bkf(bP)[:, 384:512], AF.Sigmoid, [BK[bP], TAB], [a_], bias=TAB[:, 64 + d * 4 + hp:65 + d * 4 + hp])
                a_list.append(a_)
            yield
            for hp in range(4):
                rk = RK.next()
                LD(rk[:, :, :], RW[s, 0:1536, t0:t0 + 128].rearrange("(j q) t -> q j t", q=512)[hp * 128:(hp + 1) * 128, :, :],
                   [(RW, (s, tt, hp)), (RW, (s, tt, 4 + hp)), (RW, (s, tt, 8 + hp))], [rk])
                MM(bkf(bP)[:, 0:384], sig[:, hp * 128:(hp + 1) * 128], TRIR[:, d, :], [sig, TRIR], [BK[bP]])
                ACT(e4[:, hp, :], bkf(bP)[:, 0:384], AF.Exp, [BK[bP]], [(e4, hp)])
                en = En.next()
                ACT(en[:, :], bkf(bP)[:, 0:128], AF.Exp, [BK[bP]], [en], scale=-1.0)
                for c in range(2):
                    col = c * 64 + lastc
                    CP("pool", pct[:, c, hp, 0:1], e4[:, hp, col:col + 1], [(e4, hp)], [(pct, (c, hp))])
                a_ = a_list[hp]
                tm_, kd_, ka_, pr_ = tmpv.next(), kdv.next(), kav.next(), prv.next()
                TS("pool", tm_[:, :], a_[:, :], TAB[:, 56 + hp:57 + hp], TAB2[:, 14 + hp:15 + hp], ALU.mult, ALU.add, [a_, TAB, (TAB2, "omka")], [tm_])
                TT("dve", kd_[:, :], rk[:, 1, :], tm_[:, :], ALU.mult, [rk, tm_], [kd_])
                TT("pool", ka_[:, :], rk[:, 2, :], a_[:, :], ALU.mult, [rk, a_], [ka_])
                STT(pr_[:, :], rk[:, 0, :], TAB[:, 60 + hp:61 + hp], kd_[:, :], ALU.mult, ALU.mult, [rk, TAB, kd_], [pr_])
                MM(bkf(bP)[:, 384 + hp * 2:386 + hp * 2], pr_[:, :], HSEL[:, :], [pr_, HSEL], [BK[bP]])
                CP("act", cf[:, hp * 2:hp * 2 + 2], bkf(bP)[:, 384 + hp * 2:386 + hp * 2], [BK[bP]], [(cf, hp)])

                def v3(ap):
                    return ap.rearrange("p (c t) -> p c t", c=2)
                for sl in range(2):
                    r0, r1 = sl * 64, (sl + 1) * 64
                    TT("dve", ap_[r0:r1, hp, :, 0, sl, :], v3(rk[r0:r1, 2, :]), v3(e4[r0:r1, hp, 128:256]), ALU.mult, [rk, (e4, hp)], [(ap_, hp)])
                    TT("dve", ap_[r0:r1, hp, :, 1, sl, :], v3(rk[r0:r1, 0, :]), v3(e4[r0:r1, hp, 0:128]), ALU.mult, [rk, (e4, hp)], [(ap_, hp)])
                    TT("pool", bp_[r0:r1, hp, :, sl, :], v3(ka_[r0:r1, :]), v3(en[r0:r1, :]), ALU.mult, [ka_, en], [(bp_, hp)])
                    TT("pool", kp_[r0:r1, hp, :, sl, :], v3(kd_[r0:r1, :]), v3(en[r0:r1, :]), ALU.mult, [kd_, en], [(kp_, hp)])
                    TT("pool", khp_[r0:r1, hp, :, sl, :], v3(kd_[r0:r1, :]), v3(e4[r0:r1, hp, 256:384]), ALU.mult, [kd_, (e4, hp)], [(khp_, hp)])
                    STT(bhp_[r0:r1, hp, :, sl, :], v3(ka_[r0:r1, :]), -1.0, v3(e4[r0:r1, hp, 256:384]), ALU.mult, ALU.mult, [ka_, (e4, hp)], [(bhp_, hp)])
                yield
            STO(CF[d][s, t0:t0 + 128, :], cf[:, :], [cf], [(CF[d], (s, tt))], eng="act")

        def GI(tt, c, bs, tb, out):
            ap_, bp_, kp_ = tb["ap"], tb["bp"], tb["kp"]
            tc0 = tt * 128 + c * 64
            vst = VST.next()
            out["vst"] = vst
            vsrc = VT[s, tc0:tc0 + 64, 1024:1536].rearrange("t (h s v) -> t h s v", h=4, s=2)
            for sl in range(2):
                LD(vst[sl * 64:(sl + 1) * 64, :, :], vsrc[:, :, sl, :], [(VT, (s, tt, 2))], [vst])
            MAK, ARK, ARB = MAKs[bs], ARKs[bs], ARBs[bs]
            for hp in range(4):
                MM(bkf(bH)[:, hp * 128:(hp + 1) * 128], ap_[:, hp, c, 0, :, :], bp_[:, hp, c, :, :], [(ap_, hp), (bp_, hp)], [BK[bH]])
            X, Y, H = Xm[0], Ym[0], Hm[0]
            TT("dve", Y[:, :, :], bkf(bH).rearrange("p (h t) -> p h t", h=4), MK[:, d, 4, :].rearrange("p (o t) -> p o t", o=1).broadcast_to([128, 4, 128]),
               ALU.mult, [BK[bH], MK], [Y])
            for half in range(2):
                for hp in (2 * half, 2 * half + 1):
                    o_ = (hp % 2) * 256
                    MM(bkf(bX)[:, o_:o_ + 256], bp_[:, hp, c, :, :], ap_[:, hp, c, :, :, :], [(bp_, hp), (ap_, hp)], [BK[bX]])
                    MM(bkf(bY)[:, o_:o_ + 256], kp_[:, hp, c, :, :], ap_[:, hp, c, :, :, :], [(kp_, hp), (ap_, hp)], [BK[bY]])
                g1 = bkf(bX).rearrange("p (h x t) -> p h x t", h=2, x=2)
                g2 = bkf(bY).rearrange("p (h x t) -> p h x t", h=2, x=2)
                hs_ = slice(half * 2, half * 2 + 2)
                TT("dve", X[:, hs_, :], g1[:, :, 0, :], mkb(0), ALU.mult, [BK[bX], MK], [(X, half)])
                TT("dve", ARB[:, hs_, :], g1[:, :, 1, :], mkb(1), ALU.mult, [BK[bX], MK], [(ARB, half)])
                TT("dve", MAK[:, hs_, :], g2[:, :, 0, :], mkb(2), ALU.mult, [BK[bY], MK], [(MAK, half)])
                TT("dve", ARK[:, hs_, :], g2[:, :, 1, :], mkb(3), ALU.mult, [BK[bY], MK], [(ARK, half)])
                yield
            TT("pool", H[:, :, :], X[:, :, :], idbb, ALU.add, [X, IDB], [H])
            for k in range(1, 6):
                Xn, Yn = Xm[k % 2], Ym[k % 2]
                Hn = Hm[k % 2] if k < 5 else HF[bs]
                if k <= 4:
                    for hp in range(4):
                        MM(bkf(bX)[:, hp * 128:(hp + 1) * 128], Y[:, hp, :], X[:, hp, :], [Y, X], [BK[bX]])
                for hp in range(4):
                    MM(bkf(bY)[:, hp * 128:(hp + 1) * 128], X[:, hp, :], Y[:, hp, :], [Y, X], [BK[bY]])
                if k <= 4:
                    CP("act", Xn[:, :, :], bkf(bX).rearrange("p (h t) -> p h t", h=4), [BK[bX]], [Xn])
                CP("dve", Yn[:, :, :], bkf(bY).rearrange("p (h t) -> p h t", h=4), [BK[bY]], [Yn])
                yield
                for hp in range(4):
                    MM(bkf(bH)[:, hp * 128:(hp + 1) * 128], Yn[:, hp, :], H[:, hp, :], [Yn, H], [BK[bH]])
                TT("dve", Hn[:, :, :], bkf(bH).rearrange("p (h t) -> p h t", h=4), H[:, :, :], ALU.add, [BK[bH], H], [Hn])
                yield
                X, Y, H = Xn, Yn, Hn

        def ST(tt, c, bs, tb, cb):
            ap_, khp_, bhp_, pct = tb["ap"], tb["khp"], tb["bhp"], tb["pct"]
            vst = cb["vst"]
            MAK, ARK, ARB, H = MAKs[bs], ARKs[bs], ARBs[bs], HF[bs]
            tc0 = tt * 128 + c * 64
            for hp in range(4):
                MM(bkf(bS)[:, hp * 64:(hp + 1) * 64], ap_[:, hp, c, 0, :, :], STb[:, hp, :], [(ap_, hp), STb], [BK[bS]], start=True, stop=False)
                MM(bkf(bS)[:, hp * 64:(hp + 1) * 64], MAK[:, hp, :], vst[:, hp, :], [MAK, vst], [BK[bS]], start=False, stop=True)
            CP("act", Xs[:, :, :], bkf(bS)[:, 0:256].rearrange("p (h v) -> p h v", h=4), [BK[bS]], [Xs])
            yield
            for hp in range(4):
                MM(bkf(bS)[:, hp * 64:(hp + 1) * 64], H[:, hp, :], Xs[:, hp, :], [H, Xs], [BK[bS]])
            CP("act", Us[:, :, :], bkf(bS)[:, 0:256].rearrange("p (h v) -> p h v", h=4), [BK[bS]], [Us])
            yield
            for hp in range(4):
                MM(bkf(bS)[:, hp * 64:(hp + 1) * 64], ap_[:, hp, c, 1, :, :], STb[:, hp, :], [(ap_, hp), STb], [BK[bS]], start=True, stop=False)
                MM(bkf(bS)[:, hp * 64:(hp + 1) * 64], ARK[:, hp, :], vst[:, hp, :], [ARK, vst], [BK[bS]], start=False, stop=False)
                MM(bkf(bS)[:, hp * 64:(hp + 1) * 64], ARB[:, hp, :], Us[:, hp, :], [ARB, Us], [BK[bS]], start=False, stop=True)
            oo = Oo.next()
            CP("act", oo[:, :, :], bkf(bS)[:, 0:256].rearrange("p (h v) -> p h v", h=4), [BK[bS]], [oo])
            odst = OD[d][s, tc0:tc0 + 64, 512:1024].rearrange("t (h s v) -> t h s v", h=4, s=2)
            for sl in range(2):
                STO(odst[:, :, sl, :], oo[sl * 64:(sl + 1) * 64, :, :], [oo], [(OD[d], (s, tt, 2 + c))], eng="act")
            yield
            kbv = bkb(bS).rearrange("p (h x t) -> p h x t", h=4, x=2)
            for hp in range(4):
                TRP(kbv[:, hp, 0, :], khp_[:, hp, c, :, :], [(khp_, hp)], [BK[bS]])
                TRP(kbv[:, hp, 1, :], bhp_[:, hp, c, :, :], [(bhp_, hp)], [BK[bS]])
            CP("act", KBt[:, :, :, :], kbv, [BK[bS]], [KBt])
            yield
            for hp in range(4):
                MM(bkf(bS)[:, hp * 64:(hp + 1) * 64], KBt[:, hp, 0, :], vst[:, hp, :], [KBt, vst], [BK[bS]], start=True, stop=False)
                MM(bkf(bS)[:, hp * 64:(hp + 1) * 64], KBt[:, hp, 1, :], Us[:, hp, :], [KBt, Us], [BK[bS]], start=False, stop=True)
            pcb = pct[:, c, :, 0:1].broadcast_to([128, 4, 64])
            TT("dve", STs[:, :, :], STs[:, :, :], pcb, ALU.mult, [STs, pct], [STs])
            TT("dve", STs[:, :, :], bkf(bS)[:, 0:256].rearrange("p (h v) -> p h v", h=4), STs[:, :, :], ALU.add, [BK[bS], STs], [STs])
            CP("act", STb[:, :, :], STs[:, :, :], [STs], [STb])
            yield

        seq = []
        for tt in order(d):
            for c in ((0, 1) if d == 0 else (1, 0)):
                seq.append((tt, c))
        tbufs = {}

        def front(q):
            tt, c = seq[q]
            if tt not in tbufs:
                tbufs[tt] = {}
                yield from prep(tt, tbufs[tt])
            cbufs[q] = {}
            yield from GI(tt, c, q % 2, tbufs[tt], cbufs[q])

        cbufs = {}
        for _ in front(0):
            yield
        for q in range(len(seq)):
            tt, c = seq[q]
            gens = [ST(tt, c, q % 2, tbufs[tt], cbufs[q])]
            if q + 1 < len(seq):
                gens.append(front(q + 1))
            FR = cfg.get("front_ratio", 3)
            while gens:
                for gi_, g in enumerate(list(gens)):
                    reps = FR if (gi_ == 1 and len(gens) == 2) else 1
                    for _r in range(reps):
                        try:
                            next(g)
                        except StopIteration:
                            gens.remove(g)
                            break
                        if _r + 1 < reps:
                            yield
                yield

    def stageC(l, s, ar, B):
        ar.reset([BK[b_] for b_ in B])
        last_layer = (l == L - 1)
        GROWL = ar.alloc("GROWL", [128, D], F32)
        LD(GROWL[:, :].rearrange("p (o n) -> p o n", o=1), GD[l, s:s + 1, :].partition_broadcast(128), [(GD, (l, s))], [GROWL])
        GROWC = ar.alloc("GROWC", [128, D], F32)
        LD(GROWC[:, :].rearrange("p (o n) -> p o n", o=1), GD[l, NB:NB + 1, :].partition_broadcast(128), [(GD, (l, NB))], [GROWC])
        Of = Ring([ar.alloc("Of%d" % i, [128, 1536], BF16) for i in range(2)])
        Ob = Ring([ar.alloc("Ob%d" % i, [128, 1536], BF16) for i in range(2)])
        o32 = ar.alloc("o32", [128, 1536], F32)
        sq = ar.alloc("sqc", [128, 1536], F32)
        yb = ar.alloc("yb", [128, 1536], BF16)
        gT = Ring([ar.alloc("gT%d" % i, [128, 12, 128], BF16) for i in range(1)])
        sgT = ar.alloc("sgT", [128, 12, 128], BF16)
        yT = ar.alloc("yT", [128, 12, 128], BF16)
        vr = Ring([ar.alloc("vr%d" % i, [128, 512], BF16) for i in range(1)])
        cfa = Ring([ar.alloc("cfa%d" % i, [128, 8], F32) for i in range(2)])
        cfb = Ring([ar.alloc("cfb%d" % i, [128, 8], F32) for i in range(2)])
        xt = Ring([ar.alloc("xt%d" % i, [128, D], F32) for i in range(2)])
        xo = Ring([ar.alloc("xo%d" % i, [128, D], F32) for i in range(1)])
        stt_ = Ring([ar.alloc("stc%d" % i, [128, 80], F32) for i in range(2)])
        bon = ar.alloc("bon", [128, 512], F32)
        junkC = ar.alloc("junkC", [128, 512], BF16)
        for tt in range(NT):
            isctx = tt < CT
            if isctx and last_layer:
                continue
            t0 = tt * 128
            of_, ob_ = Of.next(), Ob.next()
            LD(of_[:, :], OD[0][s, t0:t0 + 128, :], [(OD[0], (s, tt, i)) for i in range(6)], [of_])
            LD(ob_[:, :], OD[1][s, t0:t0 + 128, :], [(OD[1], (s, tt, i)) for i in range(6)], [ob_])
            g_ = gT.next()
            LD(g_[:, :, :], PF[s, PF_GATE:PF_GATE + 1536, t0:t0 + 128].rearrange("(c p) t -> p c t", p=128),
               [kPF(s, tt, PF_GATE // 128 + i) for i in range(12)], [g_])
            v_ = vr.next()
            LD(v_[:, :], VT[s, t0:t0 + 128, 1024:1536], [(VT, (s, tt, 2))], [v_])
            ca, cb = cfa.next(), cfb.next()
            LD(ca[:, :], CF[0][s, t0:t0 + 128, :], [(CF[0], (s, tt))], [ca])
            LD(cb[:, :], CF[1][s, t0:t0 + 128, :], [(CF[1], (s, tt))], [cb])
            x_ = xt.next()
            if l == 0:
                if isctx:
                    LD(x_[:, :], ctx_in[s, t0:t0 + 128, :], [ctx_in], [x_])
                else:
                    LD(x_[:, :], x_in[s, t0 - CTX:t0 - CTX + 128, :], [x_in], [x_])
            else:
                LD(x_[:, :], xs[s, t0:t0 + 128, :], [(xs, (s, tt))], [x_])
            st = stt_.next()
            TT("dve", o32[:, :], of_[:, :], ob_[:, :], ALU.add, [of_, ob_], [o32])
            ACT(sq[:, :], o32[:, :], AF.Square, [o32], [sq])
            def red(dst, src, R, W):
                P.op("dve", lambda e: e.tensor_reduce(out=dst, in_=src, axis=AX.X, op=ALU.add), R, W)
            red(st[:, 0:4], o32[:, 0:512].rearrange("p (h e) -> p h e", h=4), [o32], [(st, "s1a")])
            red(st[:, 4:12], o32[:, 512:1024].rearrange("p (h e) -> p h e", h=8), [o32], [(st, "s1b")])
            red(st[:, 12:16], sq[:, 0:512].rearrange("p (h e) -> p h e", h=4), [sq], [(st, "s2a")])
            red(st[:, 16:24], sq[:, 512:1024].rearrange("p (h e) -> p h e", h=8), [sq], [(st, "s2b")])
            red(st[:, 24:28], sq[:, 1024:1536].rearrange("p (h e) -> p h e", h=4), [sq], [(st, "s2c")])
            TS("dve", st[:, 28:32], st[:, 0:4], 1.0 / 128, None, ALU.mult, None, [(st, "s1a")], [(st, "m")])
            TS("dve", st[:, 32:40], st[:, 4:12], 1.0 / 64, None, ALU.mult, None, [(st, "s1b"), (st, "m")], [(st, "m")])
            TS("dve", st[:, 40:44], st[:, 12:16], 1.0 / 128, None, ALU.mult, None, [(st, "s2a")], [(st, "e")])
            TS("dve", st[:, 44:52], st[:, 16:24], 1.0 / 64, None, ALU.mult, None, [(st, "s2b"), (st, "e")], [(st, "e")])
            TS("dve", st[:, 52:56], st[:, 24:28], 1.0 / 128, None, ALU.mult, None, [(st, "s2c"), (st, "e")], [(st, "e")])
            TT("dve", st[:, 56:60], st[:, 28:32], st[:, 28:32], ALU.mult, [(st, "m")], [(st, "mm")])
            TT("dve", st[:, 60:64], st[:, 32:36], st[:, 32:36], ALU.mult, [(st, "m"), (st, "mm")], [(st, "mm")])
            TT("dve", st[:, 40:44], st[:, 40:44], st[:, 56:60], ALU.subtract, [(st, "e"), (st, "mm")], [(st, "e")])
            TT("dve", st[:, 44:48], st[:, 44:48], st[:, 60:64], ALU.subtract, [(st, "e"), (st, "mm")], [(st, "e")])
            TT("dve", st[:, 56:60], st[:, 36:40], st[:, 36:40], ALU.mult, [(st, "m"), (st, "mm"), (st, "e")], [(st, "mm")])
            TT("dve", st[:, 48:52], st[:, 48:52], st[:, 56:60], ALU.subtract, [(st, "e"), (st, "mm")], [(st, "e")])
            ACT(st[:, 40:44], st[:, 40:44], AF.Sqrt, [(st, "e"), EPS], [(st, "e")], bias=EPS[:, 0:1])
            ACT(st[:, 44:52], st[:, 44:52], AF.Sqrt, [(st, "e"), EPS], [(st, "e")], bias=EPS[:, 1:2])
            ACT(st[:, 52:56], st[:, 52:56], AF.Sqrt, [(st, "e"), EPS], [(st, "e")], bias=EPS[:, 0:1])
            P.op("dve", lambda e, o=st[:, 40:56], i_=st[:, 40:56]: e.reciprocal(out=o, in_=i_), [(st, "e")], [(st, "e")])

            def bc(ap, h, e):
                return ap.rearrange("p (h o) -> p h o", o=1).broadcast_to([128, h, e])
            o3 = o32[:, 0:512].rearrange("p (h e) -> p h e", h=4)
            TT("dve", o3, o3, bc(st[:, 28:32], 4, 128), ALU.subtract, [o32, (st, "m")], [(o32, "r")])
            TT("dve", o3, o3, bc(st[:, 40:44], 4, 128), ALU.mult, [(o32, "r"), (st, "e")], [(o32, "r")])
            o3 = o32[:, 512:1024].rearrange("p (h e) -> p h e", h=8)
            TT("dve", o3, o3, bc(st[:, 32:40], 8, 64), ALU.subtract, [o32, (st, "m")], [(o32, "w")])
            TT("dve", o3, o3, bc(st[:, 44:52], 8, 64), ALU.mult, [(o32, "w"), (st, "e")], [(o32, "w")])
            o3 = o32[:, 1024:1536].rearrange("p (h e) -> p h e", h=4)
            TT("dve", o3, o3, bc(st[:, 52:56], 4, 128), ALU.mult, [o32, (st, "e")], [(o32, "g")])
            TT("dve", o32[:, :], o32[:, :], ROWS[:, 0:1536], ALU.mult, [o32, ROWS], [o32])
            TT("dve", o32[:, 512:1024], o32[:, 512:1024], ROWS[:, 1536:2048], ALU.add, [o32, ROWS], [(o32, "w")])
            TT("dve", ca[:, :], ca[:, :], cb[:, :], ALU.add, [ca, cb], [ca])
            TT("dve", bon[:, :].rearrange("p (h e) -> p h e", h=8), v_[:, :].rearrange("p (h e) -> p h e", h=8), bc(ca[:, 0:8], 8, 64), ALU.mult, [v_, ca], [bon])
            TT("dve", yb[:, 512:1024], o32[:, 512:1024], bon[:, :], ALU.add, [o32, bon], [(yb, 1)])
            CP("act", yb[:, 0:512], o32[:, 0:512], [o32], [(yb, 0)])
            CP("act", yb[:, 1024:1536], o32[:, 1024:1536], [o32], [(yb, 2)])
            ACT(sgT[:, :, :], g_[:, :, :], AF.Silu, [g_], [sgT])
            yield
            ytv = [bkb(B[0]).rearrange("p (c t) -> p c t", c=8), bkb(B[1]).rearrange("p (c t) -> p c t", c=8)]
            for c in range(12):
                TRP(ytv[c // 8][:, c % 8, :], yb[:, c * 128:(c + 1) * 128], [yb], [(BK[B[c // 8]], c % 8)])
            TT("dve", yT[:, 0:8, :], ytv[0][:, 0:8, :], sgT[:, 0:8, :], ALU.mult, [BK[B[0]], sgT], [(yT, 0)])
            TT("dve", yT[:, 8:12, :], ytv[1][:, 0:4, :], sgT[:, 8:12, :], ALU.mult, [BK[B[1]], sgT], [(yT, 1)])
            for n in range(2):
                for c in range(12):
                    MM(bkf(B[2 + n])[:, :], yT[:, c, :], WOUT[:, c, n * 512:(n + 1) * 512], [yT, WOUT], [BK[B[2 + n]]], start=(c == 0), stop=(c == 11))
            yield
            for n in range(2):
                ACT(junkC[:, :], bkf(B[2 + n])[:, :], AF.Square, [BK[B[2 + n]]], [junkC, (st, ("z", n))], accum=st[:, 68 + n:69 + n])
            TT("dve", st[:, 70:71], st[:, 68:69], st[:, 69:70], ALU.add, [(st, ("z", 0)), (st, ("z", 1))], [(st, "zz")])
            ACT(st[:, 71:72], st[:, 70:71], AF.Sqrt, [(st, "zz"), EPS], [(st, "zs")], bias=EPS[:, 0:1], scale=1.0 / D)
            P.op("dve", lambda e, o=st[:, 72:73], i_=st[:, 71:72]: e.reciprocal(out=o, in_=i_), [(st, "zs")], [(st, "zr")])
            xo_ = xo.next()
            grow = GROWC if isctx else GROWL
            for n in range(2):
                cs = slice(n * 512, (n + 1) * 512)
                STT(xo_[:, cs], bkf(B[2 + n])[:, :], st[:, 72:73], grow[:, cs], ALU.mult, ALU.mult, [BK[B[2 + n]], (st, "zr"), grow], [(xo_, n)])
                TT("pool", xo_[:, cs], xo_[:, cs], x_[:, cs], ALU.add, [(xo_, n), x_], [(xo_, n)])
            if last_layer:
                STO(y_out[s, t0 - CTX:t0 - CTX + 128, :], xo_[:, :], [xo_], [(y_out, (s, tt))], is_out=True)
            else:
                STO(xs[s, t0:t0 + 128, :], xo_[:, :], [xo_], [(xs, (s, tt))])
            yield

    stages = cfg.get("stages", "ABRC")

    def pipeline(l, s, ar, B):
        if "B" in stages:
            yield from stageG(l, s, 0, ar, B)
            yield from stageG(l, s, 1, ar, B)
        if "R" in stages:
            yield from stageRP(l, s, ar, B)
            yield from stageR(l, s, 0, ar, B)
            yield from stageR(l, s, 1, ar, B)
        if "C" in stages:
            yield from stageC(l, s, ar, B)

    for l in range(L):
        layer_setup(l)
        for s in range(NB):
            if "A" in stages:
                stageA(l, s)
        ilv = cfg.get("interleave", True)
        for s0 in range(0, NB, 2):
            if ilv and s0 + 1 < NB:
                gens = [pipeline(l, s0, AR, [0, 1, 2, 3]), pipeline(l, s0 + 1, AR2, [4, 5, 6, 7])]
            else:
                gens = [pipeline(l, s0, AR, [0, 1, 2, 3])]
                if s0 + 1 < NB:
                    gens.append(None)
            if len(gens) == 2 and gens[1] is None:
                for _ in gens[0]:
                    pass
                for _ in pipeline(l, s0 + 1, AR, [0, 1, 2, 3]):
                    pass
                continue
            while gens:
                for g in list(gens):
                    try:
                        next(g)
                    except StopIteration:
                        gens.remove(g)
    if not P.out_dmas:
        STO(y_out[0, 0:1, 0:8], JUNK[0:1, 0:8], [JUNK], [y_out], is_out=True)
    P.emit()
    return nc


def make_in_maps(inp, cfg, n_cores):
    SEQ, CTX, L, NB = cfg["SEQ"], cfg["CTX"], cfg["L"], cfg["NB"]
    consts = host_consts(SEQ)
    prm = host_params(inp, L)
    maps = []
    for c in range(n_cores):
        b0 = c * NB
        cv = np.concatenate([inp["c"][b0:b0 + NB], inp["c_ctx"][None, :]], axis=0).astype(np.float32)
        m = {
            "x": np.ascontiguousarray(inp["x"][b0:b0 + NB], np.float32),
            "ctx": np.ascontiguousarray(inp["ctx"][b0:b0 + NB], np.float32),
            "cT": np.ascontiguousarray(cv.reshape(NB + 1, KD, 128).transpose(2, 1, 0)),
            "w_mod": np.ascontiguousarray(inp["w_mod"][:L], np.float32),
            "w_in": np.ascontiguousarray(inp["w_in"][:L], np.float32),
            "w_out": np.ascontiguousarray(inp["w_out"][:L], np.float32),
        }
        m.update(prm)
        for k, v in consts.items():
            m["c_" + k] = v
        maps.append(m)
    return maps


_NC_CACHE = {}


def kernel(**inputs):
    inp = {k: np.asarray(v) for k, v in inputs.items()}
    B, SEQ, _ = inp["x"].shape
    CTX = inp["ctx"].shape[1]
    L = inp["w_in"].shape[0]
    n_cores = 8
    NB = B // n_cores
    cfg = dict(SEQ=SEQ, CTX=CTX, L=L, NB=NB)
    key = (SEQ, CTX, L, NB)
    if key not in _NC_CACHE:
        _NC_CACHE[key] = build(cfg)
    nc = _NC_CACHE[key]
    maps = make_in_maps(inp, cfg, n_cores)
    res = run_bass_kernel_spmd(nc, maps, core_ids=list(range(n_cores)))
    out = np.concatenate([np.asarray(r["y"]) for r in res.results], axis=0)
    return out.astype(np.float32)
```
